# Optimizing a Trainium2 kernel written in Bass

```python
import jax, jax.numpy as jnp
from jax import lax
import numpy as np

D_MODEL = 4096
BATCH = 1
SEQ = 16384
DEPTH = 1
DEC_BATCH = 32
DEC_SEQ = 32
PAST_LEN = 1024

CHUNK = 64
N_PAST_CHUNKS = 8
BAND_PAST = N_PAST_CHUNKS * CHUNK
BAND = BAND_PAST + CHUNK
D_MIX = D_MODEL
D_CONV = D_MIX // 2
D_ATTN = D_MIX - D_CONV
HEAD_DIM = 128
N_HEADS = D_ATTN // HEAD_DIM
CONV_WIDTH = 3
MAX_REL = 256
NORM_EPS = 1e-6
ATTN_SCALE = HEAD_DIM ** -0.5
ADA_INIT = 0.2
IN_SPLITS = (D_CONV, 2 * D_CONV, 3 * D_CONV, 4 * D_CONV,
             4 * D_CONV + D_ATTN, 4 * D_CONV + 2 * D_ATTN, 4 * D_CONV + 3 * D_ATTN)
W_IN_COLS = 4 * D_CONV + 4 * D_ATTN

kernel_name = "hybrid_conv_chunkattn_stream_step"


def rmsnorm(x, g):
    xf = x.astype(jnp.float32)
    xf = xf * lax.rsqrt(jnp.mean(xf * xf, axis=-1, keepdims=True) + NORM_EPS)
    return xf.astype(x.dtype) * g


def branch_inputs(x, c, g_norm, w_ada, b_ada, w_in):
    mod = (c @ w_ada + b_ada)[:, None, :]
    shift, scale, gate = jnp.split(mod, 3, axis=-1)
    h = rmsnorm(x, g_norm) * (1 + scale) + shift
    parts = jnp.split(h @ w_in, IN_SPLITS, axis=-1)
    return parts, gate


def depthwise_conv(u_ext, conv_w, conv_b):
    t = u_ext.shape[1] - (CONV_WIDTH - 1)
    out = conv_b
    for i in range(CONV_WIDTH):
        out = out + conv_w[i] * u_ext[:, i:i + t]
    return out


def rel_bias_lookup(rel_bias, rel):
    return rel_bias[:, jnp.clip(rel, -MAX_REL, MAX_REL) + MAX_REL]


def band_attention(q, k, v, bias, mask):
    s = jnp.einsum('bqhd,bkhd->bhqk', q, k).astype(jnp.float32) * ATTN_SCALE + bias.astype(jnp.float32)
    if mask is not None:
        s = jnp.where(mask, s, -jnp.inf)
    p = jax.nn.softmax(s, axis=-1).astype(v.dtype)
    return jnp.einsum('bhqk,bkhd->bqhd', p, v)


def merge(x, a, o, z_conv, z_attn, gate, w_out):
    mixed = jnp.concatenate([a * jax.nn.silu(z_conv), o * jax.nn.silu(z_attn)], axis=-1)
    return x + gate * (mixed @ w_out)


def prompt_layer(x, c, g_norm, w_ada, b_ada, w_in, conv_w, conv_b, rel_bias, w_out):
    bsz, seq, _ = x.shape
    (xin, bg, cg, z_conv, q, k, v, z_attn), gate = branch_inputs(x, c, g_norm, w_ada, b_ada, w_in)
    u = cg * xin
    u_ext = jnp.pad(u, ((0, 0), (CONV_WIDTH - 1, 0), (0, 0)))
    a = bg * depthwise_conv(u_ext, conv_w, conv_b)
    n_chunks = seq // CHUNK
    q, k, v = (t.reshape(bsz, seq, N_HEADS, HEAD_DIM) for t in (q, k, v))
    pad = ((0, 0), (BAND_PAST, 0), (0, 0), (0, 0))
    kp, vp = jnp.pad(k, pad), jnp.pad(v, pad)
    offs = jnp.arange(BAND)
    bias = rel_bias_lookup(rel_bias, jnp.arange(CHUNK)[:, None] + BAND_PAST - offs[None, :])
    q_chunks = jnp.moveaxis(q.reshape(bsz, n_chunks, CHUNK, N_HEADS, HEAD_DIM), 1, 0)

    def one_chunk(args):
        ci, qc = args
        start = ci * CHUNK
        kb = lax.dynamic_slice_in_dim(kp, start, BAND, axis=1)
        vb = lax.dynamic_slice_in_dim(vp, start, BAND, axis=1)
        mask = (start - BAND_PAST + offs >= 0)[None, :]
        return band_attention(qc, kb, vb, bias, mask)

    o = lax.map(one_chunk, (jnp.arange(n_chunks), q_chunks))
    o = jnp.moveaxis(o, 0, 1).reshape(bsz, seq, D_ATTN)
    x_new = merge(x, a, o, z_conv, z_attn, gate, w_out)
    rows = min(BAND_PAST, seq)
    return x_new, k[:, seq - rows:], v[:, seq - rows:], u[:, seq - (CONV_WIDTH - 1):]


def sample_layer(x, c, cache_k, cache_v, cache_conv, g_norm, w_ada, b_ada, w_in, conv_w, conv_b, rel_bias, w_out):
    bsz, t, _ = x.shape
    (xin, bg, cg, z_conv, q, k, v, z_attn), gate = branch_inputs(x, c, g_norm, w_ada, b_ada, w_in)
    u = cg * xin
    u_ext = jnp.concatenate([cache_conv, u], axis=1)
    a = bg * depthwise_conv(u_ext, conv_w, conv_b)
    q, k, v = (z.reshape(bsz, t, N_HEADS, HEAD_DIM) for z in (q, k, v))
    r = cache_k.shape[1]
    kc = jnp.concatenate([cache_k, k], axis=1)
    vc = jnp.concatenate([cache_v, v], axis=1)
    bias = rel_bias_lookup(rel_bias, r + jnp.arange(t)[:, None] - jnp.arange(r + t)[None, :])
    o = band_attention(q, kc, vc, bias, None).reshape(bsz, t, D_ATTN)
    x_new = merge(x, a, o, z_conv, z_attn, gate, w_out)
    return x_new, k, v, u_ext[:, -(CONV_WIDTH - 1):]


def setup_inputs(seed: int = 0) -> dict:
    key = jax.random.key(seed)
    ks = jax.random.split(key, 16)
    f32 = jnp.float32
    r = min(BAND_PAST, PAST_LEN)
    nrm = lambda k, s: jax.random.normal(k, s, f32)
    return {
        'x_prompt': nrm(ks[0], (BATCH, SEQ, D_MODEL)),
        'x_sample': nrm(ks[1], (DEC_BATCH, DEC_SEQ, D_MODEL)),
        'cache_k': nrm(ks[2], (DEPTH, DEC_BATCH, r, N_HEADS, HEAD_DIM)),
        'cache_v': nrm(ks[3], (DEPTH, DEC_BATCH, r, N_HEADS, HEAD_DIM)),
        'cache_conv': nrm(ks[4], (DEPTH, DEC_BATCH, CONV_WIDTH - 1, D_CONV)),
        'c_prompt': nrm(ks[5], (BATCH, D_MODEL)),
        'c_sample': nrm(ks[6], (DEC_BATCH, D_MODEL)),
        'g_norm': 1.0 + 0.01 * nrm(ks[7], (DEPTH, D_MODEL)),
        'w_ada': ADA_INIT * D_MODEL ** -0.5 * nrm(ks[8], (DEPTH, D_MODEL, 3 * D_MODEL)),
        'b_ada': 0.02 * nrm(ks[9], (DEPTH, 3 * D_MODEL)),
        'w_in': D_MODEL ** -0.5 * nrm(ks[10], (DEPTH, D_MODEL, W_IN_COLS)),
        'conv_w': CONV_WIDTH ** -0.5 * nrm(ks[11], (DEPTH, CONV_WIDTH, D_CONV)),
        'conv_b': 0.01 * nrm(ks[12], (DEPTH, D_CONV)),
        'rel_bias': 0.5 * nrm(ks[13], (DEPTH, N_HEADS, 2 * MAX_REL + 1)),
        'w_out': D_MIX ** -0.5 * nrm(ks[14], (DEPTH, D_MIX, D_MODEL)),
        'g_final': 1.0 + 0.01 * nrm(ks[15], (D_MODEL,)),
    }


def reference(x_prompt, x_sample, cache_k, cache_v, cache_conv, c_prompt, c_sample,
              g_norm, w_ada, b_ada, w_in, conv_w, conv_b, rel_bias, w_out, g_final):
    xp, xs = x_prompt, x_sample
    kp_l, vp_l, up_l, ks_l, vs_l, us_l = [], [], [], [], [], []
    for l in range(DEPTH):
        xp, kp, vp, up = prompt_layer(xp, c_prompt, g_norm[l], w_ada[l], b_ada[l], w_in[l],
                                      conv_w[l], conv_b[l], rel_bias[l], w_out[l])
        xs, kn, vn, un = sample_layer(xs, c_sample, cache_k[l], cache_v[l], cache_conv[l],
                                      g_norm[l], w_ada[l], b_ada[l], w_in[l],
                                      conv_w[l], conv_b[l], rel_bias[l], w_out[l])
        kp_l.append(kp); vp_l.append(vp); up_l.append(up)
        ks_l.append(kn); vs_l.append(vn); us_l.append(un)
    y_prompt = rmsnorm(xp, g_final)
    y_sample = rmsnorm(xs, g_final)
    return (y_prompt, y_sample, jnp.stack(kp_l), jnp.stack(vp_l), jnp.stack(up_l),
            jnp.stack(ks_l), jnp.stack(vs_l), jnp.stack(us_l))
```

```python
import numpy as np
from contextlib import ExitStack
import concourse.bass as bass
import concourse.mybir as mybir
from concourse.bass_utils import run_bass_kernel_spmd

F32 = mybir.dt.float32
BF16 = mybir.dt.bfloat16
AF = mybir.ActivationFunctionType
ALU = mybir.AluOpType
AX = mybir.AxisListType

NCORES = 8
D = 4096
KT = 32
TP = 2048
HALO = 512
NBLK = 512
NPASS = TP // NBLK
TS = 128
NH = 16
EPS = 1e-6
ATTN_SCALE = 128 ** -0.5
NEG = -30000.0
SAME_ENGINE_SYNC = True
NWSLOT = 3

ENGS = ['sync', 'act', 'dve', 'pool', 'pe']


class Prog:
    def __init__(self):
        self.ops = {e: [] for e in ENGS}
        self.bufs = {}
        self.dma_count = {}

    def _deps(self, reads, writes, tok):
        deps = []
        for b in reads:
            st = self.bufs.setdefault(b, [None, []])
            if st[0] is not None:
                deps.append(st[0])
        for b in writes:
            st = self.bufs.setdefault(b, [None, []])
            if st[0] is not None:
                deps.append(st[0])
            deps.extend(st[1])
        for b in reads:
            self.bufs[b][1].append(tok)
        for b in writes:
            self.bufs[b] = [tok, []]
        return deps

    def op(self, eng, name, reads=(), writes=(), **kw):
        idx = len(self.ops[eng])
        tok = ('c', eng, idx)
        deps = self._deps(list(reads), list(writes), tok)
        fn = (lambda name, kw: lambda e: getattr(e, name)(**kw))(name, kw)
        self.ops[eng].append(dict(fn=fn, deps=deps, sig=False, dma=None))

    def dma(self, eng, key, reads=(), writes=(), **kw):
        cnt = self.dma_count.get(key, 0) + 16
        self.dma_count[key] = cnt
        tok = ('d', key, cnt)
        deps = self._deps(list(reads), list(writes), tok)
        fn = (lambda kw: lambda e: e.dma_start(**kw))(kw)
        self.ops[eng].append(dict(fn=fn, deps=deps, sig=False, dma=(key, cnt)))

    def emit(self, nc, es):
        sems = {e: es.enter_context(nc.semaphore('s_' + e)) for e in ENGS}
        dsems = {k: es.enter_context(nc.semaphore('d%d' % i)) for i, k in enumerate(self.dma_count)}
        for e in ENGS:
            for o in self.ops[e]:
                best = {}
                for d in o['deps']:
                    if d[0] == 'c':
                        if d[1] == e and (e == 'pe' or not SAME_ENGINE_SYNC):
                            continue
                        k = ('c', d[1])
                        if k not in best or best[k][2] < d[2]:
                            best[k] = d
                    else:
                        k = ('d', d[1])
                        if k not in best or best[k][2] < d[2]:
                            best[k] = d
                for d in best.values():
                    if d[0] == 'c':
                        self.ops[d[1]][d[2]]['sig'] = True
                o['deps'] = list(best.values())
        for e in ENGS:
            c = 0
            for o in self.ops[e]:
                if o['sig']:
                    c += 1
                o['sv'] = c
        final = [('d', k, v) for k, v in self.dma_count.items()]
        block = es.enter_context(nc.Block())

        def run(ename, eng):
            waited = {}
            for o in self.ops[ename]:
                for d in o['deps']:
                    if d[0] == 'c':
                        key, sem, val = ('c', d[1]), sems[d[1]], self.ops[d[1]][d[2]]['sv']
                    else:
                        key, sem, val = ('d', d[1]), dsems[d[1]], d[2]
                    if waited.get(key, 0) < val:
                        eng.wait_ge(sem, val)
                        waited[key] = val
                ins = o['fn'](eng)
                if o['dma'] is not None:
                    ins.then_inc(dsems[o['dma'][0]], 16)
                elif o['sig']:
                    ins.then_inc(sems[ename], 1)
            if ename == 'sync':
                for d in final:
                    eng.wait_ge(dsems[d[1]], d[2])

        @block.sync
        def _(eng):
            run('sync', eng)

        @block.scalar
        def _(eng):
            run('act', eng)

        @block.vector
        def _(eng):
            run('dve', eng)

        @block.gpsimd
        def _(eng):
            run('pool', eng)

        @block.tensor
        def _(eng):
            run('pe', eng)


def build_nc():
    nc = bass.Bass("TRN2", target_bir_lowering=False)
    P = Prog()
    es = ExitStack()

    def din(name, shape):
        return nc.dram_tensor(name, shape, F32, kind="ExternalInput").ap()

    def dout(name, shape):
        return nc.dram_tensor(name, shape, F32, kind="ExternalOutput").ap()

    xp = din("xp", [HALO + TP, D])
    xs = din("xs", [TS, D])
    ck = din("ck", [4, 512, NH, 128])
    cv = din("cv", [4, 512, NH, 128])
    cc = din("cc", [8, 2048])
    c5T = din("c5T", [128, KT, 5])
    gn = din("gn", [128, KT])
    bada = din("bada", [128, 96])
    cw = din("cw", [128, 16, 3])
    cbias = din("cbias", [128, 16])
    gbc = din("gbc", [128, D])
    w_ada = din("w_ada", [D, 3 * D])
    w_in = din("w_in", [D, 4 * D])
    w_out = din("w_out", [D, D])
    biasP = din("biasP", [128, NH, 640])
    biasS = din("biasS", [128, NH, 160])
    hb = din("hb", [128, 2])
    ident = din("ident", [128, 128])
    onesd = din("onesd", [128, 128])

    yp = dout("yp", [TP, D])
    ys = dout("ys", [TS, D])
    kp = dout("kp", [512, 2048])
    vp = dout("vp", [512, 2048])
    up = dout("up", [2, 2048])
    ksn = dout("ksn", [TS, 2048])
    vsn = dout("vsn", [TS, 2048])
    usn = dout("usn", [8, 2048])
    kscr = nc.dram_tensor("kscr", [NH, 128, 512], BF16, kind="Internal").ap()
    vscr = nc.dram_tensor("vscr", [NH, 128, 512], BF16, kind="Internal").ap()

    def sb(name, shape, dt=F32):
        return es.enter_context(nc.sbuf_tensor(name, shape, dt))

    ps = [es.enter_context(nc.psum_tensor("ps%d" % i, [128, 512], F32)) for i in range(8)]

    hT = sb("hT", [128, KT, NBLK], BF16)
    mixT = sb("mixT", [128, KT, NBLK], BF16)
    W = [sb("W%d" % i, [128, KT, 128], BF16) for i in range(NWSLOT)]
    xst = sb("xst", [128, D])
    gbc_sb = sb("gbc_sb", [128, D])
    biasP_sb = sb("biasP_sb", [128, NH, 640], BF16)
    biasS_sb = sb("biasS_sb", [128, NH, 160], BF16)
    kbuf = [sb("kbuf%d" % i, [128, 1024], BF16) for i in range(2)]
    vbuf = [sb("vbuf%d" % i, [128, 1024], BF16) for i in range(2)]
    ident_sb = sb("ident_sb", [128, 128])
    ones_bf = sb("ones_bf", [128, 128], BF16)
    cT = sb("cT", [128, KT, 5], BF16)
    gn_sb = sb("gn_sb", [128, KT])
    bada_sb = sb("bada_sb", [128, 96])
    cw_sb = sb("cw_sb", [128, 16, 3])
    cb_sb = sb("cb_sb", [128, 16])
    hb_sb = sb("hb_sb", [128, 2])
    mod = sb("mod", [128, 96, 5])
    gmod = sb("gmod", [128, KT, 5])
    ss8 = sb("ss8", [128, 8])
    ssv = sb("ssv", [128, 4])
    sqj = sb("sqj", [128, 512])
    ucarry = sb("ucarry", [128, 16, 2])
    uout = sb("uout", [128, 16, 8])
    xs_t = [sb("xs_t0", [128, 512])] * 2
    ubuf = [sb("ubuf%d" % i, [128, 544]) for i in range(2)]
    acc = [sb("acc0", [128, 512])] * 2
    a_t = [sb("a_t0", [128, 512])] * 2
    sz = [sb("sz%d" % i, [128, 512]) for i in range(2)]
    qT = [sb("qT%d" % i, [128, 512], BF16) for i in range(2)]
    kf = sb("kf", [128, 512])
    vf = sb("vf", [128, 512])
    kst = sb("kst", [128, 512])
    vst = sb("vst", [128, 512])
    s_sb = [sb("s_sb%d" % i, [128, 640]) for i in range(2)]
    pT = [sb("pT%d" % i, [128, 640], BF16) for i in range(2)]
    rden = sb("rden", [128, 512])
    t1 = sb("t1", [128, 512])
    kc = [sb("kc%d" % i, [128, 512]) for i in range(2)]
    kTc = [sb("kTc%d" % i, [128, 512], BF16) for i in range(2)]
    vc = [sb("vc%d" % i, [128, 512], BF16) for i in range(2)]
    vnew = [sb("vnew%d" % i, [32, 512], BF16) for i in range(2)]
    uoutp = sb("uoutp", [128, 2, 16])
    cc_sb = sb("cc_sb", [8, 128])
    ust = sb("ust", [32, 128])
    yT = ubuf
    xpc = [xs_t[0], acc[0]]
    ypc = [a_t[0], rden]
    xpc_id = [('xs_t', 0), ('acc', 0)]
    ypc_id = [('a_t', 0), 'rden']
    ssq = sb("ssq", [128, 4, 32])
    rstd2 = sb("rstd2", [128, 4])

    st = dict()

    def nxt(k, n):
        v = st.get(k, 0)
        st[k] = (v + 1) % n
        return v

    def ld(key, dst, src, wid, eng='sync'):
        P.dma(eng, key, writes=[wid], out=dst, in_=src)

    ld('c0', ident_sb[:], ident[:], 'ident')
    ld('c1', gn_sb[:], gn[:], 'gn')
    ld('c2', bada_sb[:], bada[:], 'bada')
    ld('c3', cw_sb[:], cw[:], 'cw')
    ld('c4', cb_sb[:], cbias[:], 'cb')
    ld('c5', hb_sb[:], hb[:], 'hb')
    ld('c6', gbc_sb[:], gbc[:], 'gbc')
    ld('c8', cT[:], c5T[:], 'cT', eng='pool')
    ld('c9', ones_bf[:], onesd[:], 'ones', eng='pool')
    ld('c10', biasP_sb[:], biasP[:], 'biasP', eng='pool')
    ld('c11', biasS_sb[:], biasS[:], 'biasS', eng='pool')

    scr_in_parts = [nc.dram_tensor("scr_in%d" % i, [16, 128, KT * 128], BF16, kind="Internal").ap()
                    for i in range(8)]
    scr_out_parts = [nc.dram_tensor("scr_out%d" % i, [16, 128, KT * 128], BF16, kind="Internal").ap()
                     for i in range(2)]

    class _Scr:
        def __init__(self, parts):
            self.parts = parts

        def __getitem__(self, cb):
            return self.parts[cb // 16][cb % 16]

    scr = {id(w_in): _Scr(scr_in_parts), id(w_out): _Scr(scr_out_parts)}
    cached = set()

    def colblock(wsrc, cb, rhs, rhs_ids, n0, N, bank):
        slot = nxt('wslot', NWSLOT)
        ck_ = (id(wsrc), cb)
        if ck_ in cached and st.get('use_cache', True):
            P.dma('pool', ('w', slot), reads=[('scr',) + ck_], writes=[('W', slot)],
                  out=W[slot][:].rearrange("p k c -> p (k c)"), in_=scr[id(wsrc)][cb])
        else:
            src = wsrc[:, cb * 128:(cb + 1) * 128].rearrange("(kt p) c -> p kt c", p=128)
            P.dma('pool', ('w', slot), writes=[('W', slot)], out=W[slot][:], in_=src)
            if id(wsrc) in scr and ck_ not in cached:
                P.dma('sync', ('ws', slot), reads=[('W', slot)], writes=[('scr',) + ck_],
                      out=scr[id(wsrc)][cb], in_=W[slot][:].rearrange("p k c -> p (k c)"))
                cached.add(ck_)
        for kt in range(KT):
            P.op('pe', 'matmul', reads=[('W', slot)] + rhs_ids, writes=[('ps', bank)],
                 out=ps[bank][:, 0:N], lhsT=W[slot][:, kt, :], rhs=rhs[:, kt, n0:n0 + N],
                 start=(kt == 0), stop=(kt == KT - 1))

    for cb in range(96):
        bank = cb % 8
        colblock(w_ada, cb, cT, ['cT'], 0, 5, bank)
        P.op('act', 'activation', reads=[('ps', bank), 'bada'], writes=[('mod', cb)],
             out=mod[:, cb, :], in_=ps[bank][:, 0:5], func=AF.Identity, bias=bada_sb[:, cb:cb + 1], scale=1.0)
    for j in range(5):
        P.op('dve', 'scalar_tensor_tensor', reads=[('mod', c) for c in range(32, 64)] + ['gn'],
             writes=[('gmod', j)],
             out=gmod[:, :, j], in0=mod[:, 32:64, j], scalar=1.0, in1=gn_sb[:, :], op0=ALU.add, op1=ALU.mult)

    def norm_front(xsrc, r0):
        P.dma('sync', 'xst', writes=[('xst', 0), ('xst', 1)], out=xst[:], in_=xsrc[r0:r0 + 128, :])
        for c in range(8):
            P.op('act', 'activation', reads=[('xst', c // 4)], writes=[('ss8', c)],
                 out=sqj[:], in_=xst[:, c * 512:(c + 1) * 512], func=AF.Square, accum_out=ss8[:, c:c + 1])
        P.op('dve', 'tensor_reduce', reads=[('ss8', c) for c in range(8)], writes=['ssv0'],
             out=ssv[:, 0:1], in_=ss8[:], axis=AX.X, op=ALU.add)
        P.op('dve', 'tensor_scalar', reads=['ssv0'], writes=['ssv1'],
             out=ssv[:, 1:2], in0=ssv[:, 0:1], scalar1=1.0 / D, scalar2=EPS, op0=ALU.mult, op1=ALU.add)
        P.op('act', 'activation', reads=['ssv1'], writes=['ssv2'],
             out=ssv[:, 2:3], in_=ssv[:, 1:2], func=AF.Sqrt)
        P.op('dve', 'reciprocal', reads=['ssv2'], writes=['ssv3'], out=ssv[:, 3:4], in_=ssv[:, 2:3])
        P.op('act', 'activation', reads=['ssv3', ('xst', 0)], writes=[('xst', 0)],
             out=xst[:, 0:2048], in_=xst[:, 0:2048], func=AF.Identity, scale=ssv[:, 3:4], bias=0.0)
        P.op('dve', 'tensor_scalar', reads=['ssv3', ('xst', 1)], writes=[('xst', 1)],
             out=xst[:, 2048:4096], in0=xst[:, 2048:4096], scalar1=ssv[:, 3:4], scalar2=None, op0=ALU.mult)

    def norm_back(tt, sample):
        for k4 in range(8):
            bank = nxt('tb', 8)
            for j in range(4):
                kt = k4 * 4 + j
                P.op('pe', 'transpose', reads=[('xst', kt // 16), 'ident'], writes=[('ps', bank)],
                     out=ps[bank][:, j * 128:(j + 1) * 128], in_=xst[:, kt * 128:(kt + 1) * 128],
                     identity=ident_sb[:])
            for j in range(4):
                kt = k4 * 4 + j
                segs = [(0, 128, 0)] if not sample else [(b * 32, 32, 1 + b) for b in range(4)]
                for (c0, cn, mj) in segs:
                    o = hT[:, kt, tt * 128 + c0: tt * 128 + c0 + cn]
                    i = ps[bank][:, j * 128 + c0: j * 128 + c0 + cn]
                    rd = [('ps', bank), ('gmod', mj), ('mod', kt)]
                    P.op('dve', 'tensor_scalar', reads=rd, writes=[('hT', tt, kt)],
                         out=o, in0=i, scalar1=gmod[:, kt, mj:mj + 1], scalar2=mod[:, kt, mj:mj + 1],
                         op0=ALU.mult, op1=ALU.add)

    def norm_stage(xsrc, row0, ntiles, sample):
        for tt in range(ntiles):
            norm_front(xsrc, row0 + tt * 128)
            norm_back(tt, sample)

    def norm_hooks(xsrc, row0, ntiles, sample):
        hooks = {}
        step = 32 // ntiles
        for tt in range(ntiles):
            hooks.setdefault(tt * step, []).append(
                (lambda tt: lambda: norm_front(xsrc, row0 + tt * 128))(tt))
            hooks.setdefault(tt * step + min(3, step - 1), []).append(
                (lambda tt: lambda: norm_back(tt, sample))(tt))
        return hooks

    def hT_ids(ntiles):
        return [('hT', tt, kt) for tt in range(ntiles) for kt in range(KT)]

    def conv_group(g, N, ntiles, kind):
        b0 = 4 * (g % 2)
        i2 = g % 2
        hid = hT_ids(ntiles)
        sample = (kind == 'sample')
        ub = ubuf[i2]
        if kind == 'halo':
            colblock(w_in, g, hT, hid, 480, 32, b0)
            colblock(w_in, 32 + g, hT, hid, 480, 32, b0 + 2)
            P.op('act', 'activation', reads=[('ps', b0)], writes=[('xs_t', 0)],
                 out=xs_t[i2][:, 0:32], in_=ps[b0][:, 0:32], func=AF.Copy)
            P.op('dve', 'tensor_tensor', reads=[('ps', b0 + 2), ('xs_t', 0)], writes=[('ubuf', i2)],
                 out=ub[:, 0:32], in0=ps[b0 + 2][:, 0:32], in1=xs_t[i2][:, 0:32], op=ALU.mult)
            P.op('dve', 'tensor_scalar', reads=[('ubuf', i2), 'hb'], writes=[('ucarry', g)],
                 out=ucarry[:, g, :], in0=ub[:, 30:32], scalar1=hb_sb[:, 1:2], scalar2=None, op0=ALU.mult)
            return
        if sample:
            ub3 = ub[:, 0:136].rearrange("p (b t) -> p b t", t=34)
            P.dma('sync', 'cc', writes=['cc'], out=cc_sb[:], in_=cc[:, g * 128:(g + 1) * 128])
            P.op('pe', 'transpose', reads=['cc', 'ident'], writes=[('ps', b0 + 3)],
                 out=ps[b0 + 3][:, 504:512], in_=cc_sb[0:8, :], identity=ident_sb[0:8, 0:8])
            P.op('act', 'activation', reads=[('ps', b0 + 3)], writes=[('ubuf', i2)],
                 out=ub3[:, :, 0:2], in_=ps[b0 + 3][:, 504:512].rearrange("p (b t) -> p b t", t=2),
                 func=AF.Copy)
        colblock(w_in, g, hT, hid, 0, N, b0)
        colblock(w_in, 32 + g, hT, hid, 0, N, b0 + 2)
        colblock(w_in, 16 + g, hT, hid, 0, N, b0 + 1)
        colblock(w_in, 48 + g, hT, hid, 0, N, b0 + 3)
        if not sample:
            uview = lambda off: ub[:, off:off + N]
            flat = lambda t: t[:, 0:N]
            P.op('act', 'activation', reads=[('ucarry', g)], writes=[('ubuf', i2)],
                 out=ub[:, 0:2], in_=ucarry[:, g, :], func=AF.Copy)
        else:
            ub3 = ub[:, 0:136].rearrange("p (b t) -> p b t", t=34)
            uview = lambda off: ub3[:, :, off:off + 32]
            flat = lambda t: t[:, 0:N].rearrange("p (b t) -> p b t", t=32)
        P.op('act', 'activation', reads=[('ps', b0)], writes=[('xs_t', 0)],
             out=xs_t[i2][:, 0:N], in_=ps[b0][:, 0:N], func=AF.Copy)
        P.op('dve', 'tensor_tensor', reads=[('ps', b0 + 2), ('xs_t', 0), ('ubuf', i2)], writes=[('ubuf', i2)],
             out=uview(2), in0=flat(ps[b0 + 2]), in1=flat(xs_t[i2]), op=ALU.mult)
        P.op('dve', 'tensor_scalar', reads=[('ubuf', i2), 'cw', 'cb'], writes=[('acc', 0)],
             out=flat(acc[i2]), in0=uview(0), scalar1=cw_sb[:, g, 0:1], scalar2=cb_sb[:, g:g + 1],
             op0=ALU.mult, op1=ALU.add)
        for tap in (1, 2):
            P.op('dve', 'scalar_tensor_tensor', reads=[('ubuf', i2), 'cw', ('acc', 0)], writes=[('acc', 0)],
                 out=flat(acc[i2]), in0=uview(tap), scalar=cw_sb[:, g, tap:tap + 1], in1=flat(acc[i2]),
                 op0=ALU.mult, op1=ALU.add)
        P.op('dve', 'tensor_tensor', reads=[('ps', b0 + 1), ('acc', 0)], writes=[('a_t', 0)],
             out=a_t[i2][:, 0:N], in0=ps[b0 + 1][:, 0:N], in1=acc[i2][:, 0:N], op=ALU.mult)
        P.op('act', 'activation', reads=[('ps', b0 + 3)], writes=[('sz', i2)],
             out=sz[i2][:, 0:N], in_=ps[b0 + 3][:, 0:N], func=AF.Silu)
        P.op('dve', 'tensor_tensor', reads=[('a_t', 0), ('sz', i2)], writes=[('mixT', g)],
             out=mixT[:, g, 0:N], in0=a_t[i2][:, 0:N], in1=sz[i2][:, 0:N], op=ALU.mult)
        if not sample:
            P.op('act', 'activation', reads=[('ubuf', i2)], writes=[('ucarry', g)],
                 out=ucarry[:, g, :], in_=ub[:, N:N + 2], func=AF.Copy)
        else:
            P.op('act', 'activation', reads=[('ubuf', i2)], writes=[('uout', g)],
                 out=uout[:, g, :].rearrange("p (b t) -> p b t", t=2), in_=ub3[:, :, 32:34], func=AF.Copy)

    def head_proj(h, N, ntiles, kind, pidx):
        hid = hT_ids(ntiles)
        i2 = h % 2
        sample = (kind == 'sample')
        kb, vb = kbuf[i2], vbuf[i2]
        out_pass = sample or (kind == 'prompt' and pidx == NPASS - 1)
        if kind == 'prompt':
            P.dma('sync', ('kprev', i2), reads=[('kscr', h)], writes=[('kb_prev', i2)],
                  out=kb[:, 0:512], in_=kscr[h])
            P.dma('sync', ('vprev', i2), reads=[('vscr', h)], writes=[('vb_prev', i2)],
                  out=vb[:, 0:512], in_=vscr[h])
        if kind != 'halo':
            colblock(w_in, 64 + h, hT, hid, 0, N, 0)
            P.op('act', 'activation', reads=[('ps', 0)], writes=[('qT', i2)],
                 out=qT[i2][:, 0:N], in_=ps[0][:, 0:N], func=AF.Copy, scale=ATTN_SCALE)
        colblock(w_in, 80 + h, hT, hid, 0, N, 1)
        if out_pass:
            P.op('dve', 'tensor_copy', reads=[('ps', 1)], writes=['kf'], out=kf[:, 0:N], in_=ps[1][:, 0:N])
            P.op('act', 'activation', reads=['kf'], writes=[('kb_cur', i2)],
                 out=kb[:, 512:512 + N], in_=kf[:, 0:N], func=AF.Copy)
        else:
            P.op('act', 'activation', reads=[('ps', 1)], writes=[('kb_cur', i2)],
                 out=kb[:, 512:512 + N], in_=ps[1][:, 0:N], func=AF.Copy)
        colblock(w_in, 96 + h, hT, hid, 0, N, 2)
        P.op('dve', 'tensor_copy', reads=[('ps', 2)], writes=['vf'], out=vf[:, 0:N], in_=ps[2][:, 0:N])
        if kind != 'halo':
            colblock(w_in, 112 + h, hT, hid, 0, N, 3)
            P.op('act', 'activation', reads=[('ps', 3)], writes=[('sz', i2)],
                 out=sz[i2][:, 0:N], in_=ps[3][:, 0:N], func=AF.Silu)
        seg = 32 if sample else 128
        nseg = N // seg
        for s in range(nseg):
            P.op('pe', 'transpose', reads=['vf', 'ident'], writes=[('ps', 4)],
                 out=ps[4][0:seg, s * 128:(s + 1) * 128], in_=vf[:, s * seg:(s + 1) * seg], identity=ident_sb[:])
        vsrc, vsrc_id = ps[4], ('ps', 4)
        if out_pass:
            P.op('dve', 'tensor_copy', reads=[('ps', 4)], writes=['vst'],
                 out=vst[0:seg, :], in_=ps[4][0:seg, 0:512])
            vsrc, vsrc_id = vst, 'vst'
        if not sample:
            P.op('act', 'activation', reads=[vsrc_id], writes=[('vb_cur', i2)],
                 out=vb[:, 512:1024], in_=vsrc[:, 0:512], func=AF.Copy)
        else:
            P.op('act', 'activation', reads=[vsrc_id], writes=[('vnew', i2)],
                 out=vnew[i2][0:32, :], in_=vsrc[0:32, 0:512], func=AF.Copy)
        if out_pass:
            vdst = vsn if sample else vp
            kdst = ksn if sample else kp
            pat = "(b t) d -> t b d" if sample else "(tt p) d -> p tt d"
            kw = dict(t=32) if sample else dict(p=128)
            P.dma('sync', 'vst', reads=['vst'],
                  out=vdst[:, h * 128:(h + 1) * 128].rearrange(pat, **kw),
                  in_=vst[0:seg, :].rearrange("p (a d) -> p a d", d=128))
            for s in range(nseg):
                P.op('pe', 'transpose', reads=['kf', 'ident'], writes=[('ps', 5)],
                     out=ps[5][0:seg, s * 128:(s + 1) * 128], in_=kf[:, s * seg:(s + 1) * seg],
                     identity=ident_sb[:])
            P.op('act', 'activation', reads=[('ps', 5)], writes=['kst'],
                 out=kst[0:seg, :], in_=ps[5][0:seg, 0:512], func=AF.Copy)
            P.dma('sync', 'kst', reads=['kst'],
                  out=kdst[:, h * 128:(h + 1) * 128].rearrange(pat, **kw),
                  in_=kst[0:seg, :].rearrange("p (a d) -> p a d", d=128))
        if kind == 'halo' or (kind == 'prompt' and pidx < NPASS - 1):
            P.dma('sync', ('kcur', i2), reads=[('kb_cur', i2)], writes=[('kscr', h)],
                  out=kscr[h], in_=kb[:, 512:1024])
            P.dma('sync', ('vcur', i2), reads=[('vb_cur', i2)], writes=[('vscr', h)],
                  out=vscr[h], in_=vb[:, 512:1024])

    def attn_epilogue(h, N):
        i2 = h % 2
        P.op('dve', 'reciprocal', reads=[('ps', 7)], writes=['rden'], out=rden[:, 0:N], in_=ps[7][:, 0:N])
        P.op('dve', 'tensor_tensor', reads=[('ps', 6), 'rden'], writes=['t1'],
             out=t1[:, 0:N], in0=ps[6][:, 0:N], in1=rden[:, 0:N], op=ALU.mult)
        P.op('dve', 'tensor_tensor', reads=['t1', ('sz', i2)], writes=[('mixT', 16 + h)],
             out=mixT[:, 16 + h, 0:N], in0=t1[:, 0:N], in1=sz[i2][:, 0:N], op=ALU.mult)

    def head_attn_prompt(h, pidx):
        i2 = h % 2
        kb, vb = kbuf[i2], vbuf[i2]
        kv_ids = [('kb_prev', i2), ('kb_cur', i2)]
        vv_ids = [('vb_prev', i2), ('vb_cur', i2)]
        for j in range(4):
            si = nxt('si', 2)
            for t in range(5):
                bank, col = (4, t * 128) if t < 4 else (5, 0)
                P.op('pe', 'matmul', reads=kv_ids + [('qT', i2)], writes=[('ps', bank)],
                     out=ps[bank][:, col:col + 128], lhsT=kb[:, (j + t) * 128:(j + t + 1) * 128],
                     rhs=qT[i2][:, j * 128:(j + 1) * 128], start=True, stop=True)
            nhalo = max(0, 4 - j) if pidx == 0 else 0
            segs = []
            if nhalo > 0:
                segs.append((0, min(nhalo, 4), True))
            if nhalo < 4:
                segs.append((nhalo, 4, False))
            segs.append((4, 5, False))
            for (t0, t1_, halo) in segs:
                bank, col = (4, t0 * 128) if t0 < 4 else (5, 0)
                n = (t1_ - t0) * 128
                o = s_sb[si][:, t0 * 128:t0 * 128 + n]
                i = ps[bank][:, col:col + n]
                bi = biasP_sb[:, h, t0 * 128:t0 * 128 + n]
                if halo:
                    P.op('dve', 'scalar_tensor_tensor', reads=[('ps', bank), 'biasP', 'hb'], writes=[('s_sb', si)],
                         out=o, in0=i, scalar=hb_sb[:, 0:1], in1=bi, op0=ALU.add, op1=ALU.add)
                else:
                    P.op('dve', 'tensor_tensor', reads=[('ps', bank), 'biasP'], writes=[('s_sb', si)],
                         out=o, in0=i, in1=bi, op=ALU.add)
            P.op('act', 'activation', reads=[('s_sb', si)], writes=[('pT', si)],
                 out=pT[si][:], in_=s_sb[si][:], func=AF.Exp)
            for t in range(5):
                P.op('pe', 'matmul', reads=vv_ids + [('pT', si)], writes=[('ps', 6)],
                     out=ps[6][:, j * 128:(j + 1) * 128], lhsT=vb[:, (j + t) * 128:(j + t + 1) * 128],
                     rhs=pT[si][:, t * 128:(t + 1) * 128], start=(t == 0), stop=(t == 4))
                P.op('pe', 'matmul', reads=['ones', ('pT', si)], writes=[('ps', 7)],
                     out=ps[7][:, j * 128:(j + 1) * 128], lhsT=ones_bf[:],
                     rhs=pT[si][:, t * 128:(t + 1) * 128], start=(t == 0), stop=(t == 4))
        attn_epilogue(h, NBLK)

    def head_attn_sample(h):
        i2 = h % 2
        kb = kbuf[i2]
        for b in range(4):
            ci = nxt('ci', 2)
            si = nxt('si', 2)
            P.dma('sync', ('kc', ci), writes=[('kc', ci)],
                  out=kc[ci][:].rearrange("p (t d) -> p t d", d=128),
                  in_=ck[b, :, h, :].rearrange("(t p) d -> p t d", p=128))
            P.dma('pool', ('vc', ci), writes=[('vc', ci)],
                  out=vc[ci][:].rearrange("p (t d) -> p t d", d=128),
                  in_=cv[b, :, h, :].rearrange("(t p) d -> p t d", p=128))
            for t in range(4):
                P.op('pe', 'transpose', reads=[('kc', ci), 'ident'], writes=[('ps', 5)],
                     out=ps[5][:, t * 128:(t + 1) * 128], in_=kc[ci][:, t * 128:(t + 1) * 128],
                     identity=ident_sb[:])
            P.op('act', 'activation', reads=[('ps', 5)], writes=[('kTc', ci)],
                 out=kTc[ci][:], in_=ps[5][:, 0:512], func=AF.Copy)
            qs = qT[i2][:, b * 32:(b + 1) * 32]
            for t in range(4):
                P.op('pe', 'matmul', reads=[('kTc', ci), ('qT', i2)], writes=[('ps', 4)],
                     out=ps[4][:, t * 32:(t + 1) * 32], lhsT=kTc[ci][:, t * 128:(t + 1) * 128], rhs=qs,
                     start=True, stop=True)
            P.op('pe', 'matmul', reads=[('kb_cur', i2), ('qT', i2)], writes=[('ps', 4)],
                 out=ps[4][0:32, 128:160], lhsT=kb[:, 512 + b * 32:512 + (b + 1) * 32], rhs=qs,
                 start=True, stop=True)
            P.op('dve', 'tensor_tensor', reads=[('ps', 4), 'biasS'], writes=[('s_sb', si)],
                 out=s_sb[si][:, 0:128], in0=ps[4][:, 0:128], in1=biasS_sb[:, h, 0:128], op=ALU.add)
            P.op('dve', 'tensor_tensor', reads=[('ps', 4), 'biasS'], writes=[('s_sb', si)],
                 out=s_sb[si][0:32, 128:160], in0=ps[4][0:32, 128:160], in1=biasS_sb[0:32, h, 128:160], op=ALU.add)
            P.op('act', 'activation', reads=[('s_sb', si)], writes=[('pT', si)],
                 out=pT[si][:, 0:128], in_=s_sb[si][:, 0:128], func=AF.Exp)
            P.op('act', 'activation', reads=[('s_sb', si)], writes=[('pT', si)],
                 out=pT[si][0:32, 128:160], in_=s_sb[si][0:32, 128:160], func=AF.Exp)
            for t in range(5):
                if t < 4:
                    lv = vc[ci][:, t * 128:(t + 1) * 128]
                    lo = ones_bf[:]
                    r = pT[si][:, t * 32:(t + 1) * 32]
                else:
                    lv = vnew[i2][0:32, b * 128:(b + 1) * 128]
                    lo = ones_bf[0:32, :]
                    r = pT[si][0:32, 128:160]
                P.op('pe', 'matmul', reads=[('vc', ci), ('vnew', i2), ('pT', si)], writes=[('ps', 6)],
                     out=ps[6][:, b * 32:(b + 1) * 32], lhsT=lv, rhs=r, start=(t == 0), stop=(t == 4))
                P.op('pe', 'matmul', reads=['ones', ('pT', si)], writes=[('ps', 7)],
                     out=ps[7][:, b * 32:(b + 1) * 32], lhsT=lo, rhs=r, start=(t == 0), stop=(t == 4))
        attn_epilogue(h, TS)

    def out_proj(N, ntiles, xsrc, xrow0, ydst, yrow0, sample, hooks=None):
        mids = [('mixT', k) for k in range(KT)]
        for cb in range(KT):
            for hk in (hooks or {}).get(cb, []):
                hk()
            bank = cb % 4
            yi = nxt('yi', 2)
            colblock(w_out, cb, mixT, mids, 0, N, bank)
            segs = [(0, N, 0)] if not sample else [(b * 32, 32, 1 + b) for b in range(4)]
            for (c0, cn, mj) in segs:
                P.op('act', 'activation', reads=[('ps', bank), ('mod', 64 + cb)], writes=[('ubuf', yi)],
                     out=yT[yi][:, c0:c0 + cn], in_=ps[bank][:, c0:c0 + cn], func=AF.Identity,
                     scale=mod[:, 64 + cb, mj:mj + 1], bias=0.0)
            tbk = 4 + (cb % 4)
            for tt in range(ntiles):
                P.op('pe', 'transpose', reads=[('ubuf', yi), 'ident'], writes=[('ps', tbk)],
                     out=ps[tbk][:, tt * 128:(tt + 1) * 128], in_=yT[yi][:, tt * 128:(tt + 1) * 128],
                     identity=ident_sb[:])
            P.dma('sync', ('xpc', yi), writes=[xpc_id[yi]],
                  out=xpc[yi][:, 0:N].rearrange("p (a c) -> p a c", c=128),
                  in_=xsrc[xrow0:xrow0 + N, cb * 128:(cb + 1) * 128].rearrange("(a p) c -> p a c", p=128))
            P.op('dve', 'tensor_tensor', reads=[('ps', tbk), xpc_id[yi]], writes=[ypc_id[yi]],
                 out=ypc[yi][:, 0:N], in0=ps[tbk][:, 0:N], in1=xpc[yi][:, 0:N], op=ALU.add)
            for tt in range(ntiles):
                P.op('act', 'activation', reads=[ypc_id[yi]], writes=[('ssq', tt, cb)],
                     out=sqj[:, 0:128], in_=ypc[yi][:, tt * 128:(tt + 1) * 128], func=AF.Square,
                     accum_out=ssq[:, tt, cb:cb + 1])
            P.dma('sync', ('ypc', yi), reads=[ypc_id[yi]], writes=[('ydram', cb)],
                  out=ydst[yrow0:yrow0 + N, cb * 128:(cb + 1) * 128].rearrange("(a p) c -> p a c", p=128),
                  in_=ypc[yi][:, 0:N].rearrange("p (a c) -> p a c", c=128))
        for tt in range(ntiles):
            r2 = rstd2[:, tt:tt + 1]
            P.op('dve', 'tensor_reduce', reads=[('ssq', tt, c) for c in range(KT)], writes=[('r2a', tt)],
                 out=r2, in_=ssq[:, tt, :], axis=AX.X, op=ALU.add)
            P.op('dve', 'tensor_scalar', reads=[('r2a', tt)], writes=[('r2a', tt)],
                 out=r2, in0=r2, scalar1=1.0 / D, scalar2=EPS, op0=ALU.mult, op1=ALU.add)
            P.op('act', 'activation', reads=[('r2a', tt)], writes=[('r2a', tt)], out=r2, in_=r2, func=AF.Sqrt)
            P.op('dve', 'reciprocal', reads=[('r2a', tt)], writes=[('r2a', tt)], out=r2, in_=r2)
            r0 = yrow0 + tt * 128
            P.dma('sync', 'xst', reads=[('ydram', c) for c in range(KT)], writes=[('xst', 0), ('xst', 1)],
                  out=xst[:], in_=ydst[r0:r0 + 128, :])
            P.op('dve', 'scalar_tensor_tensor', reads=[('xst', 0), ('xst', 1), ('r2a', tt), 'gbc'],
                 writes=[('xst', 0), ('xst', 1)],
                 out=xst[:], in0=xst[:], scalar=r2, in1=gbc_sb[:], op0=ALU.mult, op1=ALU.mult)
            P.dma('sync', 'yfin', reads=[('xst', 0), ('xst', 1)], writes=[('ydram', c) for c in range(KT)],
                  out=ydst[r0:r0 + 128, :], in_=xst[:])

    def program():
        norm_stage(xp, 0, 4, False)
        for g in range(16):
            conv_group(g, NBLK, 4, 'halo')
        for h in range(NH):
            head_proj(h, NBLK, 4, 'halo', -1)
        norm_stage(xp, HALO, 4, False)
        for pidx in range(NPASS):
            row0 = HALO + pidx * NBLK
            for g in range(16):
                conv_group(g, NBLK, 4, 'prompt')
            head_proj(0, NBLK, 4, 'prompt', pidx)
            for h in range(NH):
                if h + 1 < NH:
                    head_proj(h + 1, NBLK, 4, 'prompt', pidx)
                head_attn_prompt(h, pidx)
            if pidx + 1 < NPASS:
                hooks = norm_hooks(xp, row0 + NBLK, 4, False)
            else:
                hooks = norm_hooks(xs, 0, 1, True)
            out_proj(NBLK, 4, xp, row0, yp, pidx * NBLK, False, hooks)
            if pidx == NPASS - 1:
                for g in range(16):
                    P.op('act', 'activation', reads=[('ucarry', g)], writes=[('uoutp', g)],
                         out=uoutp[:, :, g], in_=ucarry[:, g, :], func=AF.Copy)
                P.op('pe', 'transpose', reads=[('uoutp', g) for g in range(16)] + ['ident'], writes=[('ps', 0)],
                     out=ps[0][0:32, 0:128], in_=uoutp[:, :, :], identity=ident_sb[:])
                P.op('act', 'activation', reads=[('ps', 0)], writes=['ust'],
                     out=ust[0:32, 0:128], in_=ps[0][0:32, 0:128], func=AF.Copy)
                for t in range(2):
                    P.dma('sync', 'ust', reads=['ust'],
                          out=up[t, :].rearrange("(g p) -> g p", p=128), in_=ust[t * 16:(t + 1) * 16, 0:128])
        for g in range(16):
            conv_group(g, TS, 1, 'sample')
        head_proj(0, TS, 1, 'sample', -1)
        for h in range(NH):
            if h + 1 < NH:
                head_proj(h + 1, TS, 1, 'sample', -1)
            head_attn_sample(h)
        out_proj(TS, 1, xs, 0, ys, 0, True)
        for g4 in range(4):
            P.op('pe', 'transpose', reads=[('uout', g) for g in range(16)] + ['ident'], writes=[('ps', 0)],
                 out=ps[0][0:32, g4 * 128:(g4 + 1) * 128], in_=uout[:, g4 * 4:(g4 + 1) * 4, :],
                 identity=ident_sb[:])
        P.op('act', 'activation', reads=[('ps', 0)], writes=['kst'],
             out=kst[0:32, 0:512], in_=ps[0][0:32, 0:512], func=AF.Copy)
        for g in range(16):
            g4, gl = g // 4, g % 4
            P.dma('sync', 'kst', reads=['kst'],
                  out=usn[:, g * 128:(g + 1) * 128], in_=kst[gl * 8:(gl + 1) * 8, g4 * 128:(g4 + 1) * 128])

    program()
    P.emit(nc, es)
    es.close()
    return nc


def _bias_tiles(rel_bias):
    rb = rel_bias[0]
    p = np.arange(128)[:, None, None]
    t = np.arange(5)[None, :, None]
    q = np.arange(128)[None, None, :]
    rel = q - ((t - 4) * 128 + p)
    idx = np.clip(rel, -256, 256) + 256
    bp = rb[:, idx]
    qc = q // 64
    kc = ((t - 4) * 128 + p) // 64
    visible = (kc <= qc) & (kc >= qc - 8)
    bp = np.where(visible[None], bp, np.float32(NEG)).astype(np.float32)
    biasP = np.ascontiguousarray(bp.transpose(1, 0, 2, 3)).reshape(128, 16, 640)
    j = (np.arange(5)[None, :, None] * 128 + np.arange(128)[:, None, None])
    tq = np.arange(32)[None, None, :]
    rel_s = 512 + tq - j
    idx_s = np.clip(rel_s, -256, 256) + 256
    bs = rb[:, idx_s]
    bs = np.where((j < 544)[None], bs, np.float32(NEG)).astype(np.float32)
    biasS = np.ascontiguousarray(bs.transpose(1, 0, 2, 3)).reshape(128, 16, 160)
    return biasP, biasS


_NC_CACHE = {}


def kernel(x_prompt, x_sample, cache_k, cache_v, cache_conv, c_prompt, c_sample,
           g_norm, w_ada, b_ada, w_in, conv_w, conv_b, rel_bias, w_out, g_final):
    f32 = np.float32
    x_prompt = np.asarray(x_prompt, f32); x_sample = np.asarray(x_sample, f32)
    cache_k = np.asarray(cache_k, f32); cache_v = np.asarray(cache_v, f32)
    cache_conv = np.asarray(cache_conv, f32)
    w_ada2 = np.ascontiguousarray(np.asarray(w_ada, f32)[0])
    w_in2 = np.ascontiguousarray(np.asarray(w_in, f32)[0])
    w_out2 = np.ascontiguousarray(np.asarray(w_out, f32)[0])
    fm = lambda v, n: np.ascontiguousarray(np.asarray(v, f32).reshape(n, 128).T)
    gn = fm(g_norm[0], 32)
    bada = fm(b_ada[0], 96)
    cw = np.ascontiguousarray(np.asarray(conv_w, f32)[0].reshape(3, 16, 128).transpose(2, 1, 0))
    cbias = fm(conv_b[0], 16)
    gbc = np.ascontiguousarray(np.broadcast_to(np.asarray(g_final, f32)[None, :], (128, D)))
    biasP, biasS = _bias_tiles(np.asarray(rel_bias, f32))
    ident = np.eye(128, dtype=f32)
    onesd = np.ones((128, 128), f32)
    xfull = np.concatenate([np.zeros((HALO, D), f32), x_prompt[0]], axis=0)

    in_maps = []
    for c in range(NCORES):
        c5 = np.concatenate([np.asarray(c_prompt, f32), np.asarray(c_sample, f32)[4 * c:4 * c + 4]], axis=0)
        c5T = np.ascontiguousarray(c5.T.reshape(32, 128, 5).transpose(1, 0, 2))
        hbv = np.zeros((128, 2), f32)
        hbv[:, 0] = NEG if c == 0 else 0.0
        hbv[:, 1] = 0.0 if c == 0 else 1.0
        in_maps.append(dict(
            xp=np.ascontiguousarray(xfull[c * TP: c * TP + HALO + TP]),
            xs=np.ascontiguousarray(x_sample[4 * c:4 * c + 4].reshape(TS, D)),
            ck=np.ascontiguousarray(cache_k[0, 4 * c:4 * c + 4]),
            cv=np.ascontiguousarray(cache_v[0, 4 * c:4 * c + 4]),
            cc=np.ascontiguousarray(cache_conv[0, 4 * c:4 * c + 4].reshape(8, 2048)),
            c5T=c5T, gn=gn, bada=bada, cw=cw, cbias=cbias, gbc=gbc,
            w_ada=w_ada2, w_in=w_in2, w_out=w_out2, biasP=biasP, biasS=biasS, hb=hbv,
            ident=ident, onesd=onesd))
    if 'nc' not in _NC_CACHE:
        _NC_CACHE['nc'] = build_nc()
    res = run_bass_kernel_spmd(_NC_CACHE['nc'], in_maps, core_ids=list(range(NCORES)))
    R = res.results
    y_prompt = np.concatenate([R[c]["yp"] for c in range(NCORES)], axis=0)[None]
    y_sample = np.concatenate([R[c]["ys"].reshape(4, 32, D) for c in range(NCORES)], axis=0)
    new_k_prompt = R[NCORES - 1]["kp"].reshape(1, 1, 512, NH, 128)
    new_v_prompt = R[NCORES - 1]["vp"].reshape(1, 1, 512, NH, 128)
    new_conv_prompt = R[NCORES - 1]["up"].reshape(1, 1, 2, 2048)
    new_k_sample = np.concatenate([R[c]["ksn"].reshape(4, 32, NH, 128) for c in range(NCORES)], axis=0)[None]
    new_v_sample = np.concatenate([R[c]["vsn"].reshape(4, 32, NH, 128) for c in range(NCORES)], axis=0)[None]
    new_conv_sample = np.concatenate([R[c]["usn"].reshape(4, 2, 2048) for c in range(NCORES)], axis=0)[None]
    return tuple(np.ascontiguousarray(a, dtype=f32) for a in (
        y_prompt, y_sample, new_k_prompt, new_v_prompt, new_conv_prompt,
        new_k_sample, new_v_sample, new_conv_sample))
```

```python
import numpy as np
from contextlib import ExitStack
import concourse.bass as bass
import concourse.mybir as mybir
from concourse.bass_utils import run_bass_kernel_spmd

F32 = mybir.dt.float32
BF16 = mybir.dt.bfloat16
AF = mybir.ActivationFunctionType
ALU = mybir.AluOpType
AX = mybir.AxisListType

NCORES = 8
D = 4096
KT = 32
TP = 2048
HALO = 512
NBLK = 512
NPASS = TP // NBLK
TS = 128
NH = 16
EPS = 1e-6
ATTN_SCALE = 128 ** -0.5
NEG = -30000.0
SAME_ENGINE_SYNC = True
NWSLOT = 3

ENGS = ['sync', 'act', 'dve', 'pool', 'pe']


class Prog:
    def __init__(self):
        self.ops = {e: [] for e in ENGS}
        self.bufs = {}
        self.dma_count = {}

    def _deps(self, reads, writes, tok):
        deps = []
        for b in reads:
            st = self.bufs.setdefault(b, [None, []])
            if st[0] is not None:
                deps.append(st[0])
        for b in writes:
            st = self.bufs.setdefault(b, [None, []])
            if st[0] is not None:
                deps.append(st[0])
            deps.extend(st[1])
        for b in reads:
            self.bufs[b][1].append(tok)
        for b in writes:
            self.bufs[b] = [tok, []]
        return deps

    def op(self, eng, name, reads=(), writes=(), **kw):
        idx = len(self.ops[eng])
        tok = ('c', eng, idx)
        deps = self._deps(list(reads), list(writes), tok)
        fn = (lambda name, kw: lambda e: getattr(e, name)(**kw))(name, kw)
        self.ops[eng].append(dict(fn=fn, deps=deps, sig=False, dma=None))

    def dma(self, eng, key, reads=(), writes=(), **kw):
        cnt = self.dma_count.get(key, 0) + 16
        self.dma_count[key] = cnt
        tok = ('d', key, cnt)
        deps = self._deps(list(reads), list(writes), tok)
        fn = (lambda kw: lambda e: e.dma_start(**kw))(kw)
        self.ops[eng].append(dict(fn=fn, deps=deps, sig=False, dma=(key, cnt)))

    def emit(self, nc, es):
        sems = {e: es.enter_context(nc.semaphore('s_' + e)) for e in ENGS}
        dsems = {k: es.enter_context(nc.semaphore('d%d' % i)) for i, k in enumerate(self.dma_count)}
        for e in ENGS:
            for o in self.ops[e]:
                best = {}
                for d in o['deps']:
                    if d[0] == 'c':
                        if d[1] == e and (e == 'pe' or not SAME_ENGINE_SYNC):
                            continue
                        k = ('c', d[1])
                        if k not in best or best[k][2] < d[2]:
                            best[k] = d
                    else:
                        k = ('d', d[1])
                        if k not in best or best[k][2] < d[2]:
                            best[k] = d
                for d in best.values():
                    if d[0] == 'c':
                        self.ops[d[1]][d[2]]['sig'] = True
                o['deps'] = list(best.values())
        for e in ENGS:
            c = 0
            for o in self.ops[e]:
                if o['sig']:
                    c += 1
                o['sv'] = c
        final = [('d', k, v) for k, v in self.dma_count.items()]
        block = es.enter_context(nc.Block())

        def run(ename, eng):
            waited = {}
            for o in self.ops[ename]:
                for d in o['deps']:
                    if d[0] == 'c':
                        key, sem, val = ('c', d[1]), sems[d[1]], self.ops[d[1]][d[2]]['sv']
                    else:
                        key, sem, val = ('d', d[1]), dsems[d[1]], d[2]
                    if waited.get(key, 0) < val:
                        eng.wait_ge(sem, val)
                        waited[key] = val
                ins = o['fn'](eng)
                if o['dma'] is not None:
                    ins.then_inc(dsems[o['dma'][0]], 16)
                elif o['sig']:
                    ins.then_inc(sems[ename], 1)
            if ename == 'sync':
                for d in final:
                    eng.wait_ge(dsems[d[1]], d[2])

        @block.sync
        def _(eng):
            run('sync', eng)

        @block.scalar
        def _(eng):
            run('act', eng)

        @block.vector
        def _(eng):
            run('dve', eng)

        @block.gpsimd
        def _(eng):
            run('pool', eng)

        @block.tensor
        def _(eng):
            run('pe', eng)


def build_nc():
    nc = bass.Bass("TRN2", target_bir_lowering=False)
    P = Prog()
    es = ExitStack()

    def din(name, shape):
        return nc.dram_tensor(name, shape, F32, kind="ExternalInput").ap()

    def dout(name, shape):
        return nc.dram_tensor(name, shape, F32, kind="ExternalOutput").ap()

    xp = din("xp", [HALO + TP, D])
    xs = din("xs", [TS, D])
    ck = din("ck", [4, 512, NH, 128])
    cv = din("cv", [4, 512, NH, 128])
    cc = din("cc", [8, 2048])
    c5T = din("c5T", [128, KT, 5])
    gn = din("gn", [128, KT])
    bada = din("bada", [128, 96])
    cw = din("cw", [128, 16, 3])
    cbias = din("cbias", [128, 16])
    gbc = din("gbc", [128, D])
    w_ada = din("w_ada", [96, 128, KT * 128])
    w_in = din("w_in", [128, 128, KT * 128])
    w_out = din("w_out", [KT, 128, KT * 128])
    biasP = din("biasP", [128, NH, 640])
    biasS = din("biasS", [128, NH, 160])
    hb = din("hb", [128, 2])
    ident = din("ident", [128, 128])
    onesd = din("onesd", [128, 128])

    yp = dout("yp", [TP, D])
    ys = dout("ys", [TS, D])
    kp = dout("kp", [512, 2048])
    vp = dout("vp", [512, 2048])
    up = dout("up", [2, 2048])
    ksn = dout("ksn", [TS, 2048])
    vsn = dout("vsn", [TS, 2048])
    usn = dout("usn", [8, 2048])
    kscr = nc.dram_tensor("kscr", [NH, 128, 512], BF16, kind="Internal").ap()
    vscr = nc.dram_tensor("vscr", [NH, 128, 512], BF16, kind="Internal").ap()

    def sb(name, shape, dt=F32):
        return es.enter_context(nc.sbuf_tensor(name, shape, dt))

    ps = [es.enter_context(nc.psum_tensor("ps%d" % i, [128, 512], F32)) for i in range(8)]

    hT = sb("hT", [128, KT, NBLK], BF16)
    mixT = sb("mixT", [128, KT, NBLK], BF16)
    W = [sb("W%d" % i, [128, KT, 128], BF16) for i in range(NWSLOT)]
    xst = sb("xst", [128, D])
    gbc_sb = sb("gbc_sb", [128, D])
    biasP_sb = sb("biasP_sb", [128, NH, 640], BF16)
    biasS_sb = sb("biasS_sb", [128, NH, 160], BF16)
    kbuf = [sb("kbuf%d" % i, [128, 1024], BF16) for i in range(2)]
    vbuf = [sb("vbuf%d" % i, [128, 1024], BF16) for i in range(2)]
    ident_sb = sb("ident_sb", [128, 128])
    ones_bf = sb("ones_bf", [128, 128], BF16)
    cT = sb("cT", [128, KT, 5], BF16)
    gn_sb = sb("gn_sb", [128, KT])
    bada_sb = sb("bada_sb", [128, 96])
    cw_sb = sb("cw_sb", [128, 16, 3])
    cb_sb = sb("cb_sb", [128, 16])
    hb_sb = sb("hb_sb", [128, 2])
    mod = sb("mod", [128, 96, 5])
    gmod = sb("gmod", [128, KT, 5])
    ss8 = sb("ss8", [128, 8])
    ssv = sb("ssv", [128, 4])
    sqj = sb("sqj", [128, 512])
    ucarry = sb("ucarry", [128, 16, 2])
    uout = sb("uout", [128, 16, 8])
    xs_t = [sb("xs_t0", [128, 512])] * 2
    ubuf = [sb("ubuf%d" % i, [128, 544]) for i in range(2)]
    acc = [sb("acc0", [128, 512])] * 2
    a_t = [sb("a_t0", [128, 512])] * 2
    sz = [sb("sz%d" % i, [128, 512]) for i in range(2)]
    qT = [sb("qT%d" % i, [128, 512], BF16) for i in range(2)]
    kf = sb("kf", [128, 512])
    vf = sb("vf", [128, 512])
    kst = sb("kst", [128, 512])
    vst = sb("vst", [128, 512])
    s_sb = [sb("s_sb%d" % i, [128, 640]) for i in range(2)]
    pT = [sb("pT%d" % i, [128, 640], BF16) for i in range(2)]
    rden = sb("rden", [128, 512])
    t1 = sb("t1", [128, 512])
    kc = [sb("kc%d" % i, [128, 512]) for i in range(2)]
    kTc = [sb("kTc%d" % i, [128, 512], BF16) for i in range(2)]
    vc = [sb("vc%d" % i, [128, 512], BF16) for i in range(2)]
    vnew = [sb("vnew%d" % i, [32, 512], BF16) for i in range(2)]
    uoutp = sb("uoutp", [128, 2, 16])
    cc_sb = sb("cc_sb", [8, 128])
    ust = sb("ust", [32, 128])
    yT = ubuf
    xpc = [xs_t[0], acc[0]]
    ypc = [a_t[0], rden]
    xpc_id = [('xs_t', 0), ('acc', 0)]
    ypc_id = [('a_t', 0), 'rden']
    ssq = sb("ssq", [128, 4, 32])
    rstd2 = sb("rstd2", [128, 4])

    st = dict()

    def nxt(k, n):
        v = st.get(k, 0)
        st[k] = (v + 1) % n
        return v

    def ld(key, dst, src, wid, eng='sync'):
        P.dma(eng, key, writes=[wid], out=dst, in_=src)

    ld('c0', ident_sb[:], ident[:], 'ident')
    ld('c1', gn_sb[:], gn[:], 'gn')
    ld('c2', bada_sb[:], bada[:], 'bada')
    ld('c3', cw_sb[:], cw[:], 'cw')
    ld('c4', cb_sb[:], cbias[:], 'cb')
    ld('c5', hb_sb[:], hb[:], 'hb')
    ld('c6', gbc_sb[:], gbc[:], 'gbc')
    ld('c8', cT[:], c5T[:], 'cT', eng='pool')
    ld('c9', ones_bf[:], onesd[:], 'ones', eng='pool')
    ld('c10', biasP_sb[:], biasP[:], 'biasP', eng='pool')
    ld('c11', biasS_sb[:], biasS[:], 'biasS', eng='pool')

    scr_in_parts = [nc.dram_tensor("scr_in%d" % i, [16, 128, KT * 128], BF16, kind="Internal").ap()
                    for i in range(8)]
    scr_out_parts = [nc.dram_tensor("scr_out%d" % i, [16, 128, KT * 128], BF16, kind="Internal").ap()
                     for i in range(2)]

    class _Scr:
        def __init__(self, parts):
            self.parts = parts

        def __getitem__(self, cb):
            return self.parts[cb // 16][cb % 16]

    scr = {id(w_in): _Scr(scr_in_parts), id(w_out): _Scr(scr_out_parts)}
    cached = set()

    def colblock(wsrc, cb, rhs, rhs_ids, n0, N, bank):
        slot = nxt('wslot', NWSLOT)
        ck_ = (id(wsrc), cb)
        if ck_ in cached and st.get('use_cache', True):
            P.dma('pool', ('w', slot), reads=[('scr',) + ck_], writes=[('W', slot)],
                  out=W[slot][:].rearrange("p k c -> p (k c)"), in_=scr[id(wsrc)][cb])
        else:
            P.dma('pool', ('w', slot), writes=[('W', slot)],
                  out=W[slot][:].rearrange("p (a k) c -> p a (k c)", a=2),
                  in_=wsrc[cb].rearrange("p (a f) -> p a f", a=2))
            if id(wsrc) in scr and ck_ not in cached:
                P.dma('sync', ('ws', slot), reads=[('W', slot)], writes=[('scr',) + ck_],
                      out=scr[id(wsrc)][cb], in_=W[slot][:].rearrange("p k c -> p (k c)"))
                cached.add(ck_)
        for kt in range(KT):
            P.op('pe', 'matmul', reads=[('W', slot)] + rhs_ids, writes=[('ps', bank)],
                 out=ps[bank][:, 0:N], lhsT=W[slot][:, kt, :], rhs=rhs[:, kt, n0:n0 + N],
                 start=(kt == 0), stop=(kt == KT - 1))

    for cb in range(96):
        bank = cb % 8
        colblock(w_ada, cb, cT, ['cT'], 0, 5, bank)
        P.op('act', 'activation', reads=[('ps', bank), 'bada'], writes=[('mod', cb)],
             out=mod[:, cb, :], in_=ps[bank][:, 0:5], func=AF.Identity, bias=bada_sb[:, cb:cb + 1], scale=1.0)
    for j in range(5):
        P.op('dve', 'scalar_tensor_tensor', reads=[('mod', c) for c in range(32, 64)] + ['gn'],
             writes=[('gmod', j)],
             out=gmod[:, :, j], in0=mod[:, 32:64, j], scalar=1.0, in1=gn_sb[:, :], op0=ALU.add, op1=ALU.mult)

    def norm_front(xsrc, r0):
        P.dma('sync', 'xst', writes=[('xst', 0), ('xst', 1)], out=xst[:], in_=xsrc[r0:r0 + 128, :])
        for c in range(8):
            P.op('act', 'activation', reads=[('xst', c // 4)], writes=[('ss8', c)],
                 out=sqj[:], in_=xst[:, c * 512:(c + 1) * 512], func=AF.Square, accum_out=ss8[:, c:c + 1])
        P.op('dve', 'tensor_reduce', reads=[('ss8', c) for c in range(8)], writes=['ssv0'],
             out=ssv[:, 0:1], in_=ss8[:], axis=AX.X, op=ALU.add)
        P.op('dve', 'tensor_scalar', reads=['ssv0'], writes=['ssv1'],
             out=ssv[:, 1:2], in0=ssv[:, 0:1], scalar1=1.0 / D, scalar2=EPS, op0=ALU.mult, op1=ALU.add)
        P.op('act', 'activation', reads=['ssv1'], writes=['ssv2'],
             out=ssv[:, 2:3], in_=ssv[:, 1:2], func=AF.Sqrt)
        P.op('dve', 'reciprocal', reads=['ssv2'], writes=['ssv3'], out=ssv[:, 3:4], in_=ssv[:, 2:3])
        P.op('act', 'activation', reads=['ssv3', ('xst', 0)], writes=[('xst', 0)],
             out=xst[:, 0:2048], in_=xst[:, 0:2048], func=AF.Identity, scale=ssv[:, 3:4], bias=0.0)
        P.op('dve', 'tensor_scalar', reads=['ssv3', ('xst', 1)], writes=[('xst', 1)],
             out=xst[:, 2048:4096], in0=xst[:, 2048:4096], scalar1=ssv[:, 3:4], scalar2=None, op0=ALU.mult)

    def norm_back(tt, sample):
        for k4 in range(8):
            bank = nxt('tb', 8)
            for j in range(4):
                kt = k4 * 4 + j
                P.op('pe', 'transpose', reads=[('xst', kt // 16), 'ident'], writes=[('ps', bank)],
                     out=ps[bank][:, j * 128:(j + 1) * 128], in_=xst[:, kt * 128:(kt + 1) * 128],
                     identity=ident_sb[:])
            for j in range(4):
                kt = k4 * 4 + j
                segs = [(0, 128, 0)] if not sample else [(b * 32, 32, 1 + b) for b in range(4)]
                for (c0, cn, mj) in segs:
                    o = hT[:, kt, tt * 128 + c0: tt * 128 + c0 + cn]
                    i = ps[bank][:, j * 128 + c0: j * 128 + c0 + cn]
                    rd = [('ps', bank), ('gmod', mj), ('mod', kt)]
                    P.op('dve', 'tensor_scalar', reads=rd, writes=[('hT', tt, kt)],
                         out=o, in0=i, scalar1=gmod[:, kt, mj:mj + 1], scalar2=mod[:, kt, mj:mj + 1],
                         op0=ALU.mult, op1=ALU.add)

    def norm_stage(xsrc, row0, ntiles, sample):
        for tt in range(ntiles):
            norm_front(xsrc, row0 + tt * 128)
            norm_back(tt, sample)

    def norm_hooks(xsrc, row0, ntiles, sample):
        hooks = {}
        step = 32 // ntiles
        for tt in range(ntiles):
            hooks.setdefault(tt * step, []).append(
                (lambda tt: lambda: norm_front(xsrc, row0 + tt * 128))(tt))
            hooks.setdefault(tt * step + min(3, step - 1), []).append(
                (lambda tt: lambda: norm_back(tt, sample))(tt))
        return hooks

    def hT_ids(ntiles):
        return [('hT', tt, kt) for tt in range(ntiles) for kt in range(KT)]

    def conv_group(g, N, ntiles, kind):
        b0 = 4 * (g % 2)
        i2 = g % 2
        hid = hT_ids(ntiles)
        sample = (kind == 'sample')
        ub = ubuf[i2]
        if kind == 'halo':
            colblock(w_in, g, hT, hid, 480, 32, b0)
            colblock(w_in, 32 + g, hT, hid, 480, 32, b0 + 2)
            P.op('act', 'activation', reads=[('ps', b0)], writes=[('xs_t', 0)],
                 out=xs_t[i2][:, 0:32], in_=ps[b0][:, 0:32], func=AF.Copy)
            P.op('dve', 'tensor_tensor', reads=[('ps', b0 + 2), ('xs_t', 0)], writes=[('ubuf', i2)],
                 out=ub[:, 0:32], in0=ps[b0 + 2][:, 0:32], in1=xs_t[i2][:, 0:32], op=ALU.mult)
            P.op('dve', 'tensor_scalar', reads=[('ubuf', i2), 'hb'], writes=[('ucarry', g)],
                 out=ucarry[:, g, :], in0=ub[:, 30:32], scalar1=hb_sb[:, 1:2], scalar2=None, op0=ALU.mult)
            return
        if sample:
            ub3 = ub[:, 0:136].rearrange("p (b t) -> p b t", t=34)
            P.dma('sync', 'cc', writes=['cc'], out=cc_sb[:], in_=cc[:, g * 128:(g + 1) * 128])
            P.op('pe', 'transpose', reads=['cc', 'ident'], writes=[('ps', b0 + 3)],
                 out=ps[b0 + 3][:, 504:512], in_=cc_sb[0:8, :], identity=ident_sb[0:8, 0:8])
            P.op('act', 'activation', reads=[('ps', b0 + 3)], writes=[('ubuf', i2)],
                 out=ub3[:, :, 0:2], in_=ps[b0 + 3][:, 504:512].rearrange("p (b t) -> p b t", t=2),
                 func=AF.Copy)
        colblock(w_in, g, hT, hid, 0, N, b0)
        colblock(w_in, 32 + g, hT, hid, 0, N, b0 + 2)
        colblock(w_in, 16 + g, hT, hid, 0, N, b0 + 1)
        colblock(w_in, 48 + g, hT, hid, 0, N, b0 + 3)
        if not sample:
            uview = lambda off: ub[:, off:off + N]
            flat = lambda t: t[:, 0:N]
            P.op('act', 'activation', reads=[('ucarry', g)], writes=[('ubuf', i2)],
                 out=ub[:, 0:2], in_=ucarry[:, g, :], func=AF.Copy)
        else:
            ub3 = ub[:, 0:136].rearrange("p (b t) -> p b t", t=34)
            uview = lambda off: ub3[:, :, off:off + 32]
            flat = lambda t: t[:, 0:N].rearrange("p (b t) -> p b t", t=32)
        P.op('act', 'activation', reads=[('ps', b0)], writes=[('xs_t', 0)],
             out=xs_t[i2][:, 0:N], in_=ps[b0][:, 0:N], func=AF.Copy)
        P.op('dve', 'tensor_tensor', reads=[('ps', b0 + 2), ('xs_t', 0), ('ubuf', i2)], writes=[('ubuf', i2)],
             out=uview(2), in0=flat(ps[b0 + 2]), in1=flat(xs_t[i2]), op=ALU.mult)
        P.op('dve', 'tensor_scalar', reads=[('ubuf', i2), 'cw', 'cb'], writes=[('acc', 0)],
             out=flat(acc[i2]), in0=uview(0), scalar1=cw_sb[:, g, 0:1], scalar2=cb_sb[:, g:g + 1],
             op0=ALU.mult, op1=ALU.add)
        for tap in (1, 2):
            P.op('dve', 'scalar_tensor_tensor', reads=[('ubuf', i2), 'cw', ('acc', 0)], writes=[('acc', 0)],
                 out=flat(acc[i2]), in0=uview(tap), scalar=cw_sb[:, g, tap:tap + 1], in1=flat(acc[i2]),
                 op0=ALU.mult, op1=ALU.add)
        P.op('dve', 'tensor_tensor', reads=[('ps', b0 + 1), ('acc', 0)], writes=[('a_t', 0)],
             out=a_t[i2][:, 0:N], in0=ps[b0 + 1][:, 0:N], in1=acc[i2][:, 0:N], op=ALU.mult)
        P.op('act', 'activation', reads=[('ps', b0 + 3)], writes=[('sz', i2)],
             out=sz[i2][:, 0:N], in_=ps[b0 + 3][:, 0:N], func=AF.Silu)
        P.op('dve', 'tensor_tensor', reads=[('a_t', 0), ('sz', i2)], writes=[('mixT', g)],
             out=mixT[:, g, 0:N], in0=a_t[i2][:, 0:N], in1=sz[i2][:, 0:N], op=ALU.mult)
        if not sample:
            P.op('act', 'activation', reads=[('ubuf', i2)], writes=[('ucarry', g)],
                 out=ucarry[:, g, :], in_=ub[:, N:N + 2], func=AF.Copy)
        else:
            P.op('act', 'activation', reads=[('ubuf', i2)], writes=[('uout', g)],
                 out=uout[:, g, :].rearrange("p (b t) -> p b t", t=2), in_=ub3[:, :, 32:34], func=AF.Copy)

    def head_proj(h, N, ntiles, kind, pidx):
        hid = hT_ids(ntiles)
        i2 = h % 2
        sample = (kind == 'sample')
        kb, vb = kbuf[i2], vbuf[i2]
        out_pass = sample or (kind == 'prompt' and pidx == NPASS - 1)
        if kind == 'prompt':
            P.dma('sync', ('kprev', i2), reads=[('kscr', h)], writes=[('kb_prev', i2)],
                  out=kb[:, 0:512], in_=kscr[h])
            P.dma('sync', ('vprev', i2), reads=[('vscr', h)], writes=[('vb_prev', i2)],
                  out=vb[:, 0:512], in_=vscr[h])
        if kind != 'halo':
            colblock(w_in, 64 + h, hT, hid, 0, N, 0)
            P.op('act', 'activation', reads=[('ps', 0)], writes=[('qT', i2)],
                 out=qT[i2][:, 0:N], in_=ps[0][:, 0:N], func=AF.Copy, scale=ATTN_SCALE)
        colblock(w_in, 80 + h, hT, hid, 0, N, 1)
        if out_pass:
            P.op('dve', 'tensor_copy', reads=[('ps', 1)], writes=['kf'], out=kf[:, 0:N], in_=ps[1][:, 0:N])
            P.op('act', 'activation', reads=['kf'], writes=[('kb_cur', i2)],
                 out=kb[:, 512:512 + N], in_=kf[:, 0:N], func=AF.Copy)
        else:
            P.op('act', 'activation', reads=[('ps', 1)], writes=[('kb_cur', i2)],
                 out=kb[:, 512:512 + N], in_=ps[1][:, 0:N], func=AF.Copy)
        colblock(w_in, 96 + h, hT, hid, 0, N, 2)
        P.op('dve', 'tensor_copy', reads=[('ps', 2)], writes=['vf'], out=vf[:, 0:N], in_=ps[2][:, 0:N])
        if kind != 'halo':
            colblock(w_in, 112 + h, hT, hid, 0, N, 3)
            P.op('act', 'activation', reads=[('ps', 3)], writes=[('sz', i2)],
                 out=sz[i2][:, 0:N], in_=ps[3][:, 0:N], func=AF.Silu)
        seg = 32 if sample else 128
        nseg = N // seg
        for s in range(nseg):
            P.op('pe', 'transpose', reads=['vf', 'ident'], writes=[('ps', 4)],
                 out=ps[4][0:seg, s * 128:(s + 1) * 128], in_=vf[:, s * seg:(s + 1) * seg], identity=ident_sb[:])
        vsrc, vsrc_id = ps[4], ('ps', 4)
        if out_pass:
            P.op('dve', 'tensor_copy', reads=[('ps', 4)], writes=['vst'],
                 out=vst[0:seg, :], in_=ps[4][0:seg, 0:512])
            vsrc, vsrc_id = vst, 'vst'
        if not sample:
            P.op('act', 'activation', reads=[vsrc_id], writes=[('vb_cur', i2)],
                 out=vb[:, 512:1024], in_=vsrc[:, 0:512], func=AF.Copy)
        else:
            P.op('act', 'activation', reads=[vsrc_id], writes=[('vnew', i2)],
                 out=vnew[i2][0:32, :], in_=vsrc[0:32, 0:512], func=AF.Copy)
        if out_pass:
            vdst = vsn if sample else vp
            kdst = ksn if sample else kp
            pat = "(b t) d -> t b d" if sample else "(tt p) d -> p tt d"
            kw = dict(t=32) if sample else dict(p=128)
            P.dma('sync', 'vst', reads=['vst'],
                  out=vdst[:, h * 128:(h + 1) * 128].rearrange(pat, **kw),
                  in_=vst[0:seg, :].rearrange("p (a d) -> p a d", d=128))
            for s in range(nseg):
                P.op('pe', 'transpose', reads=['kf', 'ident'], writes=[('ps', 5)],
                     out=ps[5][0:seg, s * 128:(s + 1) * 128], in_=kf[:, s * seg:(s + 1) * seg],
                     identity=ident_sb[:])
            P.op('act', 'activation', reads=[('ps', 5)], writes=['kst'],
                 out=kst[0:seg, :], in_=ps[5][0:seg, 0:512], func=AF.Copy)
            P.dma('sync', 'kst', reads=['kst'],
                  out=kdst[:, h * 128:(h + 1) * 128].rearrange(pat, **kw),
                  in_=kst[0:seg, :].rearrange("p (a d) -> p a d", d=128))
        if kind == 'halo' or (kind == 'prompt' and pidx < NPASS - 1):
            P.dma('sync', ('kcur', i2), reads=[('kb_cur', i2)], writes=[('kscr', h)],
                  out=kscr[h], in_=kb[:, 512:1024])
            P.dma('sync', ('vcur', i2), reads=[('vb_cur', i2)], writes=[('vscr', h)],
                  out=vscr[h], in_=vb[:, 512:1024])

    def attn_epilogue(h, N):
        i2 = h % 2
        P.op('dve', 'reciprocal', reads=[('ps', 7)], writes=['rden'], out=rden[:, 0:N], in_=ps[7][:, 0:N])
        P.op('dve', 'tensor_tensor', reads=[('ps', 6), 'rden'], writes=['t1'],
             out=t1[:, 0:N], in0=ps[6][:, 0:N], in1=rden[:, 0:N], op=ALU.mult)
        P.op('dve', 'tensor_tensor', reads=['t1', ('sz', i2)], writes=[('mixT', 16 + h)],
             out=mixT[:, 16 + h, 0:N], in0=t1[:, 0:N], in1=sz[i2][:, 0:N], op=ALU.mult)

    def head_attn_prompt(h, pidx):
        i2 = h % 2
        kb, vb = kbuf[i2], vbuf[i2]
        kv_ids = [('kb_prev', i2), ('kb_cur', i2)]
        vv_ids = [('vb_prev', i2), ('vb_cur', i2)]
        for j in range(4):
            si = nxt('si', 2)
            for t in range(5):
                bank, col = (4, t * 128) if t < 4 else (5, 0)
                P.op('pe', 'matmul', reads=kv_ids + [('qT', i2)], writes=[('ps', bank)],
                     out=ps[bank][:, col:col + 128], lhsT=kb[:, (j + t) * 128:(j + t + 1) * 128],
                     rhs=qT[i2][:, j * 128:(j + 1) * 128], start=True, stop=True)
            nhalo = max(0, 4 - j) if pidx == 0 else 0
            segs = []
            if nhalo > 0:
                segs.append((0, min(nhalo, 4), True))
            if nhalo < 4:
                segs.append((nhalo, 4, False))
            segs.append((4, 5, False))
            for (t0, t1_, halo) in segs:
                bank, col = (4, t0 * 128) if t0 < 4 else (5, 0)
                n = (t1_ - t0) * 128
                o = s_sb[si][:, t0 * 128:t0 * 128 + n]
                i = ps[bank][:, col:col + n]
                bi = biasP_sb[:, h, t0 * 128:t0 * 128 + n]
                if halo:
                    P.op('dve', 'scalar_tensor_tensor', reads=[('ps', bank), 'biasP', 'hb'], writes=[('s_sb', si)],
                         out=o, in0=i, scalar=hb_sb[:, 0:1], in1=bi, op0=ALU.add, op1=ALU.add)
                else:
                    P.op('dve', 'tensor_tensor', reads=[('ps', bank), 'biasP'], writes=[('s_sb', si)],
                         out=o, in0=i, in1=bi, op=ALU.add)
            P.op('act', 'activation', reads=[('s_sb', si)], writes=[('pT', si)],
                 out=pT[si][:], in_=s_sb[si][:], func=AF.Exp)
            for t in range(5):
                P.op('pe', 'matmul', reads=vv_ids + [('pT', si)], writes=[('ps', 6)],
                     out=ps[6][:, j * 128:(j + 1) * 128], lhsT=vb[:, (j + t) * 128:(j + t + 1) * 128],
                     rhs=pT[si][:, t * 128:(t + 1) * 128], start=(t == 0), stop=(t == 4))
                P.op('pe', 'matmul', reads=['ones', ('pT', si)], writes=[('ps', 7)],
                     out=ps[7][:, j * 128:(j + 1) * 128], lhsT=ones_bf[:],
                     rhs=pT[si][:, t * 128:(t + 1) * 128], start=(t == 0), stop=(t == 4))
        attn_epilogue(h, NBLK)

    def head_attn_sample(h):
        i2 = h % 2
        kb = kbuf[i2]
        for b in range(4):
            ci = nxt('ci', 2)
            si = nxt('si', 2)
            P.dma('sync', ('kc', ci), writes=[('kc', ci)],
                  out=kc[ci][:].rearrange("p (t d) -> p t d", d=128),
                  in_=ck[b, :, h, :].rearrange("(t p) d -> p t d", p=128))
            P.dma('pool', ('vc', ci), writes=[('vc', ci)],
                  out=vc[ci][:].rearrange("p (t d) -> p t d", d=128),
                  in_=cv[b, :, h, :].rearrange("(t p) d -> p t d", p=128))
            for t in range(4):
                P.op('pe', 'transpose', reads=[('kc', ci), 'ident'], writes=[('ps', 5)],
                     out=ps[5][:, t * 128:(t + 1) * 128], in_=kc[ci][:, t * 128:(t + 1) * 128],
                     identity=ident_sb[:])
            P.op('act', 'activation', reads=[('ps', 5)], writes=[('kTc', ci)],
                 out=kTc[ci][:], in_=ps[5][:, 0:512], func=AF.Copy)
            qs = qT[i2][:, b * 32:(b + 1) * 32]
            for t in range(4):
                P.op('pe', 'matmul', reads=[('kTc', ci), ('qT', i2)], writes=[('ps', 4)],
                     out=ps[4][:, t * 32:(t + 1) * 32], lhsT=kTc[ci][:, t * 128:(t + 1) * 128], rhs=qs,
                     start=True, stop=True)
            P.op('pe', 'matmul', reads=[('kb_cur', i2), ('qT', i2)], writes=[('ps', 4)],
                 out=ps[4][0:32, 128:160], lhsT=kb[:, 512 + b * 32:512 + (b + 1) * 32], rhs=qs,
                 start=True, stop=True)
            P.op('dve', 'tensor_tensor', reads=[('ps', 4), 'biasS'], writes=[('s_sb', si)],
                 out=s_sb[si][:, 0:128], in0=ps[4][:, 0:128], in1=biasS_sb[:, h, 0:128], op=ALU.add)
            P.op('dve', 'tensor_tensor', reads=[('ps', 4), 'biasS'], writes=[('s_sb', si)],
                 out=s_sb[si][0:32, 128:160], in0=ps[4][0:32, 128:160], in1=biasS_sb[0:32, h, 128:160], op=ALU.add)
            P.op('act', 'activation', reads=[('s_sb', si)], writes=[('pT', si)],
                 out=pT[si][:, 0:128], in_=s_sb[si][:, 0:128], func=AF.Exp)
            P.op('act', 'activation', reads=[('s_sb', si)], writes=[('pT', si)],
                 out=pT[si][0:32, 128:160], in_=s_sb[si][0:32, 128:160], func=AF.Exp)
            for t in range(5):
                if t < 4:
                    lv = vc[ci][:, t * 128:(t + 1) * 128]
                    lo = ones_bf[:]
                    r = pT[si][:, t * 32:(t + 1) * 32]
                else:
                    lv = vnew[i2][0:32, b * 128:(b + 1) * 128]
                    lo = ones_bf[0:32, :]
                    r = pT[si][0:32, 128:160]
                P.op('pe', 'matmul', reads=[('vc', ci), ('vnew', i2), ('pT', si)], writes=[('ps', 6)],
                     out=ps[6][:, b * 32:(b + 1) * 32], lhsT=lv, rhs=r, start=(t == 0), stop=(t == 4))
                P.op('pe', 'matmul', reads=['ones', ('pT', si)], writes=[('ps', 7)],
                     out=ps[7][:, b * 32:(b + 1) * 32], lhsT=lo, rhs=r, start=(t == 0), stop=(t == 4))
        attn_epilogue(h, TS)

    def out_proj(N, ntiles, xsrc, xrow0, ydst, yrow0, sample, hooks=None):
        mids = [('mixT', k) for k in range(KT)]
        for cb in range(KT):
            for hk in (hooks or {}).get(cb, []):
                hk()
            bank = cb % 4
            yi = nxt('yi', 2)
            colblock(w_out, cb, mixT, mids, 0, N, bank)
            segs = [(0, N, 0)] if not sample else [(b * 32, 32, 1 + b) for b in range(4)]
            for (c0, cn, mj) in segs:
                P.op('act', 'activation', reads=[('ps', bank), ('mod', 64 + cb)], writes=[('ubuf', yi)],
                     out=yT[yi][:, c0:c0 + cn], in_=ps[bank][:, c0:c0 + cn], func=AF.Identity,
                     scale=mod[:, 64 + cb, mj:mj + 1], bias=0.0)
            tbk = 4 + (cb % 4)
            for tt in range(ntiles):
                P.op('pe', 'transpose', reads=[('ubuf', yi), 'ident'], writes=[('ps', tbk)],
                     out=ps[tbk][:, tt * 128:(tt + 1) * 128], in_=yT[yi][:, tt * 128:(tt + 1) * 128],
                     identity=ident_sb[:])
            P.dma('sync', ('xpc', yi), writes=[xpc_id[yi]],
                  out=xpc[yi][:, 0:N].rearrange("p (a c) -> p a c", c=128),
                  in_=xsrc[xrow0:xrow0 + N, cb * 128:(cb + 1) * 128].rearrange("(a p) c -> p a c", p=128))
            P.op('dve', 'tensor_tensor', reads=[('ps', tbk), xpc_id[yi]], writes=[ypc_id[yi]],
                 out=ypc[yi][:, 0:N], in0=ps[tbk][:, 0:N], in1=xpc[yi][:, 0:N], op=ALU.add)
            for tt in range(ntiles):
                P.op('act', 'activation', reads=[ypc_id[yi]], writes=[('ssq', tt, cb)],
                     out=sqj[:, 0:128], in_=ypc[yi][:, tt * 128:(tt + 1) * 128], func=AF.Square,
                     accum_out=ssq[:, tt, cb:cb + 1])
            P.dma('sync', ('ypc', yi), reads=[ypc_id[yi]], writes=[('ydram', cb)],
                  out=ydst[yrow0:yrow0 + N, cb * 128:(cb + 1) * 128].rearrange("(a p) c -> p a c", p=128),
                  in_=ypc[yi][:, 0:N].rearrange("p (a c) -> p a c", c=128))
        for tt in range(ntiles):
            r2 = rstd2[:, tt:tt + 1]
            P.op('dve', 'tensor_reduce', reads=[('ssq', tt, c) for c in range(KT)], writes=[('r2a', tt)],
                 out=r2, in_=ssq[:, tt, :], axis=AX.X, op=ALU.add)
            P.op('dve', 'tensor_scalar', reads=[('r2a', tt)], writes=[('r2a', tt)],
                 out=r2, in0=r2, scalar1=1.0 / D, scalar2=EPS, op0=ALU.mult, op1=ALU.add)
            P.op('act', 'activation', reads=[('r2a', tt)], writes=[('r2a', tt)], out=r2, in_=r2, func=AF.Sqrt)
            P.op('dve', 'reciprocal', reads=[('r2a', tt)], writes=[('r2a', tt)], out=r2, in_=r2)
            r0 = yrow0 + tt * 128
            P.dma('sync', 'xst', reads=[('ydram', c) for c in range(KT)], writes=[('xst', 0), ('xst', 1)],
                  out=xst[:], in_=ydst[r0:r0 + 128, :])
            P.op('dve', 'scalar_tensor_tensor', reads=[('xst', 0), ('xst', 1), ('r2a', tt), 'gbc'],
                 writes=[('xst', 0), ('xst', 1)],
                 out=xst[:], in0=xst[:], scalar=r2, in1=gbc_sb[:], op0=ALU.mult, op1=ALU.mult)
            P.dma('sync', 'yfin', reads=[('xst', 0), ('xst', 1)], writes=[('ydram', c) for c in range(KT)],
                  out=ydst[r0:r0 + 128, :], in_=xst[:])

    def program():
        norm_stage(xp, 0, 4, False)
        for g in range(16):
            conv_group(g, NBLK, 4, 'halo')
        for h in range(NH):
            head_proj(h, NBLK, 4, 'halo', -1)
        norm_stage(xp, HALO, 4, False)
        for pidx in range(NPASS):
            row0 = HALO + pidx * NBLK
            for g in range(16):
                conv_group(g, NBLK, 4, 'prompt')
            head_proj(0, NBLK, 4, 'prompt', pidx)
            for h in range(NH):
                if h + 1 < NH:
                    head_proj(h + 1, NBLK, 4, 'prompt', pidx)
                head_attn_prompt(h, pidx)
            if pidx + 1 < NPASS:
                hooks = norm_hooks(xp, row0 + NBLK, 4, False)
            else:
                hooks = norm_hooks(xs, 0, 1, True)
            out_proj(NBLK, 4, xp, row0, yp, pidx * NBLK, False, hooks)
            if pidx == NPASS - 1:
                for g in range(16):
                    P.op('act', 'activation', reads=[('ucarry', g)], writes=[('uoutp', g)],
                         out=uoutp[:, :, g], in_=ucarry[:, g, :], func=AF.Copy)
                P.op('pe', 'transpose', reads=[('uoutp', g) for g in range(16)] + ['ident'], writes=[('ps', 0)],
                     out=ps[0][0:32, 0:128], in_=uoutp[:, :, :], identity=ident_sb[:])
                P.op('act', 'activation', reads=[('ps', 0)], writes=['ust'],
                     out=ust[0:32, 0:128], in_=ps[0][0:32, 0:128], func=AF.Copy)
                for t in range(2):
                    P.dma('sync', 'ust', reads=['ust'],
                          out=up[t, :].rearrange("(g p) -> g p", p=128), in_=ust[t * 16:(t + 1) * 16, 0:128])
        for g in range(16):
            conv_group(g, TS, 1, 'sample')
        head_proj(0, TS, 1, 'sample', -1)
        for h in range(NH):
            if h + 1 < NH:
                head_proj(h + 1, TS, 1, 'sample', -1)
            head_attn_sample(h)
        out_proj(TS, 1, xs, 0, ys, 0, True)
        for g4 in range(4):
            P.op('pe', 'transpose', reads=[('uout', g) for g in range(16)] + ['ident'], writes=[('ps', 0)],
                 out=ps[0][0:32, g4 * 128:(g4 + 1) * 128], in_=uout[:, g4 * 4:(g4 + 1) * 4, :],
                 identity=ident_sb[:])
        P.op('act', 'activation', reads=[('ps', 0)], writes=['kst'],
             out=kst[0:32, 0:512], in_=ps[0][0:32, 0:512], func=AF.Copy)
        for g in range(16):
            g4, gl = g // 4, g % 4
            P.dma('sync', 'kst', reads=['kst'],
                  out=usn[:, g * 128:(g + 1) * 128], in_=kst[gl * 8:(gl + 1) * 8, g4 * 128:(g4 + 1) * 128])

    program()
    P.emit(nc, es)
    es.close()
    return nc


def _bias_tiles(rel_bias):
    rb = rel_bias[0]
    p = np.arange(128)[:, None, None]
    t = np.arange(5)[None, :, None]
    q = np.arange(128)[None, None, :]
    rel = q - ((t - 4) * 128 + p)
    idx = np.clip(rel, -256, 256) + 256
    bp = rb[:, idx]
    qc = q // 64
    kc = ((t - 4) * 128 + p) // 64
    visible = (kc <= qc) & (kc >= qc - 8)
    bp = np.where(visible[None], bp, np.float32(NEG)).astype(np.float32)
    biasP = np.ascontiguousarray(bp.transpose(1, 0, 2, 3)).reshape(128, 16, 640)
    j = (np.arange(5)[None, :, None] * 128 + np.arange(128)[:, None, None])
    tq = np.arange(32)[None, None, :]
    rel_s = 512 + tq - j
    idx_s = np.clip(rel_s, -256, 256) + 256
    bs = rb[:, idx_s]
    bs = np.where((j < 544)[None], bs, np.float32(NEG)).astype(np.float32)
    biasS = np.ascontiguousarray(bs.transpose(1, 0, 2, 3)).reshape(128, 16, 160)
    return biasP, biasS


_NC_CACHE = {}


def kernel(x_prompt, x_sample, cache_k, cache_v, cache_conv, c_prompt, c_sample,
           g_norm, w_ada, b_ada, w_in, conv_w, conv_b, rel_bias, w_out, g_final):
    f32 = np.float32
    x_prompt = np.asarray(x_prompt, f32); x_sample = np.asarray(x_sample, f32)
    cache_k = np.asarray(cache_k, f32); cache_v = np.asarray(cache_v, f32)
    cache_conv = np.asarray(cache_conv, f32)
    def blk(w):
        n = w.shape[1] // 128
        return np.ascontiguousarray(w.reshape(KT, 128, n, 128).transpose(2, 1, 0, 3)).reshape(n, 128, KT * 128)
    w_ada2 = blk(np.asarray(w_ada, f32)[0])
    w_in2 = blk(np.asarray(w_in, f32)[0])
    w_out2 = blk(np.asarray(w_out, f32)[0])
    fm = lambda v, n: np.ascontiguousarray(np.asarray(v, f32).reshape(n, 128).T)
    gn = fm(g_norm[0], 32)
    bada = fm(b_ada[0], 96)
    cw = np.ascontiguousarray(np.asarray(conv_w, f32)[0].reshape(3, 16, 128).transpose(2, 1, 0))
    cbias = fm(conv_b[0], 16)
    gbc = np.ascontiguousarray(np.broadcast_to(np.asarray(g_final, f32)[None, :], (128, D)))
    biasP, biasS = _bias_tiles(np.asarray(rel_bias, f32))
    ident = np.eye(128, dtype=f32)
    onesd = np.ones((128, 128), f32)
    xfull = np.concatenate([np.zeros((HALO, D), f32), x_prompt[0]], axis=0)

    in_maps = []
    for c in range(NCORES):
        c5 = np.concatenate([np.asarray(c_prompt, f32), np.asarray(c_sample, f32)[4 * c:4 * c + 4]], axis=0)
        c5T = np.ascontiguousarray(c5.T.reshape(32, 128, 5).transpose(1, 0, 2))
        hbv = np.zeros((128, 2), f32)
        hbv[:, 0] = NEG if c == 0 else 0.0
        hbv[:, 1] = 0.0 if c == 0 else 1.0
        in_maps.append(dict(
            xp=np.ascontiguousarray(xfull[c * TP: c * TP + HALO + TP]),
            xs=np.ascontiguousarray(x_sample[4 * c:4 * c + 4].reshape(TS, D)),
            ck=np.ascontiguousarray(cache_k[0, 4 * c:4 * c + 4]),
            cv=np.ascontiguousarray(cache_v[0, 4 * c:4 * c + 4]),
            cc=np.ascontiguousarray(cache_conv[0, 4 * c:4 * c + 4].reshape(8, 2048)),
            c5T=c5T, gn=gn, bada=bada, cw=cw, cbias=cbias, gbc=gbc,
            w_ada=w_ada2, w_in=w_in2, w_out=w_out2, biasP=biasP, biasS=biasS, hb=hbv,
            ident=ident, onesd=onesd))
    if 'nc' not in _NC_CACHE:
        _NC_CACHE['nc'] = build_nc()
    res = run_bass_kernel_spmd(_NC_CACHE['nc'], in_maps, core_ids=list(range(NCORES)))
    R = res.results
    y_prompt = np.concatenate([R[c]["yp"] for c in range(NCORES)], axis=0)[None]
    y_sample = np.concatenate([R[c]["ys"].reshape(4, 32, D) for c in range(NCORES)], axis=0)
    new_k_prompt = R[NCORES - 1]["kp"].reshape(1, 1, 512, NH, 128)
    new_v_prompt = R[NCORES - 1]["vp"].reshape(1, 1, 512, NH, 128)
    new_conv_prompt = R[NCORES - 1]["up"].reshape(1, 1, 2, 2048)
    new_k_sample = np.concatenate([R[c]["ksn"].reshape(4, 32, NH, 128) for c in range(NCORES)], axis=0)[None]
    new_v_sample = np.concatenate([R[c]["vsn"].reshape(4, 32, NH, 128) for c in range(NCORES)], axis=0)[None]
    new_conv_sample = np.concatenate([R[c]["usn"].reshape(4, 2, 2048) for c in range(NCORES)], axis=0)[None]
    return tuple(np.ascontiguousarray(a, dtype=f32) for a in (
        y_prompt, y_sample, new_k_prompt, new_v_prompt, new_conv_prompt,
        new_k_sample, new_v_sample, new_conv_sample))
```

```python
import numpy as np
from contextlib import ExitStack
import concourse.bass as bass
import concourse.mybir as mybir
from concourse.bass_utils import run_bass_kernel_spmd

F32 = mybir.dt.float32
BF16 = mybir.dt.bfloat16
AF = mybir.ActivationFunctionType
ALU = mybir.AluOpType
AX = mybir.AxisListType

NCORES = 8
D = 4096
KT = 32
TP = 2048
HALO = 512
NBLK = 512
NPASS = TP // NBLK
TS = 128
NH = 16
EPS = 1e-6
ATTN_SCALE = 128 ** -0.5
NEG = -30000.0
SAME_ENGINE_SYNC = True
NWSLOT = 3

ENGS = ['sync', 'act', 'dve', 'pool', 'pe']


class Prog:
    def __init__(self):
        self.ops = {e: [] for e in ENGS}
        self.bufs = {}
        self.dma_count = {}

    def _deps(self, reads, writes, tok):
        deps = []
        for b in reads:
            st = self.bufs.setdefault(b, [None, []])
            if st[0] is not None:
                deps.append(st[0])
        for b in writes:
            st = self.bufs.setdefault(b, [None, []])
            if st[0] is not None:
                deps.append(st[0])
            deps.extend(st[1])
        for b in reads:
            self.bufs[b][1].append(tok)
        for b in writes:
            self.bufs[b] = [tok, []]
        return deps

    def op(self, eng, name, reads=(), writes=(), **kw):
        idx = len(self.ops[eng])
        tok = ('c', eng, idx)
        deps = self._deps(list(reads), list(writes), tok)
        fn = (lambda name, kw: lambda e: getattr(e, name)(**kw))(name, kw)
        self.ops[eng].append(dict(fn=fn, deps=deps, sig=False, dma=None))

    def dma(self, eng, key, reads=(), writes=(), **kw):
        cnt = self.dma_count.get(key, 0) + 16
        self.dma_count[key] = cnt
        tok = ('d', key, cnt)
        deps = self._deps(list(reads), list(writes), tok)
        fn = (lambda kw: lambda e: e.dma_start(**kw))(kw)
        self.ops[eng].append(dict(fn=fn, deps=deps, sig=False, dma=(key, cnt)))

    def emit(self, nc, es):
        sems = {e: es.enter_context(nc.semaphore('s_' + e)) for e in ENGS}
        dsems = {k: es.enter_context(nc.semaphore('d%d' % i)) for i, k in enumerate(self.dma_count)}
        for e in ENGS:
            for o in self.ops[e]:
                best = {}
                for d in o['deps']:
                    if d[0] == 'c':
                        if d[1] == e and (e == 'pe' or not SAME_ENGINE_SYNC):
                            continue
                        k = ('c', d[1])
                        if k not in best or best[k][2] < d[2]:
                            best[k] = d
                    else:
                        k = ('d', d[1])
                        if k not in best or best[k][2] < d[2]:
                            best[k] = d
                for d in best.values():
                    if d[0] == 'c':
                        self.ops[d[1]][d[2]]['sig'] = True
                o['deps'] = list(best.values())
        for e in ENGS:
            c = 0
            for o in self.ops[e]:
                if o['sig']:
                    c += 1
                o['sv'] = c
        final = [('d', k, v) for k, v in self.dma_count.items()]
        block = es.enter_context(nc.Block())

        def run(ename, eng):
            waited = {}
            for o in self.ops[ename]:
                for d in o['deps']:
                    if d[0] == 'c':
                        key, sem, val = ('c', d[1]), sems[d[1]], self.ops[d[1]][d[2]]['sv']
                    else:
                        key, sem, val = ('d', d[1]), dsems[d[1]], d[2]
                    if waited.get(key, 0) < val:
                        eng.wait_ge(sem, val)
                        waited[key] = val
                ins = o['fn'](eng)
                if o['dma'] is not None:
                    ins.then_inc(dsems[o['dma'][0]], 16)
                elif o['sig']:
                    ins.then_inc(sems[ename], 1)
            if ename == 'sync':
                for d in final:
                    eng.wait_ge(dsems[d[1]], d[2])

        @block.sync
        def _(eng):
            run('sync', eng)

        @block.scalar
        def _(eng):
            run('act', eng)

        @block.vector
        def _(eng):
            run('dve', eng)

        @block.gpsimd
        def _(eng):
            run('pool', eng)

        @block.tensor
        def _(eng):
            run('pe', eng)


def build_nc():
    nc = bass.Bass("TRN2", target_bir_lowering=False)
    P = Prog()
    es = ExitStack()

    def din(name, shape):
        return nc.dram_tensor(name, shape, F32, kind="ExternalInput").ap()

    def dout(name, shape):
        return nc.dram_tensor(name, shape, F32, kind="ExternalOutput").ap()

    xp = din("xp", [HALO + TP, D])
    xs = din("xs", [TS, D])
    ck = din("ck", [4, 512, NH, 128])
    cv = din("cv", [4, 512, NH, 128])
    cc = din("cc", [8, 2048])
    c5T = din("c5T", [128, KT, 5])
    gn = din("gn", [128, KT])
    bada = din("bada", [128, 96])
    cw = din("cw", [128, 16, 3])
    cbias = din("cbias", [128, 16])
    gbc = din("gbc", [128, D])
    w_ada = din("w_ada", [96, 128, KT * 128])
    w_in = din("w_in", [128, 128, KT * 128])
    w_out = din("w_out", [KT, 128, KT * 128])
    biasP = din("biasP", [128, NH, 640])
    biasS = din("biasS", [128, NH, 160])
    hb = din("hb", [128, 2])
    ident = din("ident", [128, 128])
    onesd = din("onesd", [128, 128])

    yp = dout("yp", [TP, D])
    ys = dout("ys", [TS, D])
    kp = dout("kp", [512, 2048])
    vp = dout("vp", [512, 2048])
    up = dout("up", [2, 2048])
    ksn = dout("ksn", [TS, 2048])
    vsn = dout("vsn", [TS, 2048])
    usn = dout("usn", [8, 2048])
    kscr = nc.dram_tensor("kscr", [NH, 128, 512], BF16, kind="Internal").ap()
    vscr = nc.dram_tensor("vscr", [NH, 128, 512], BF16, kind="Internal").ap()

    def sb(name, shape, dt=F32):
        return es.enter_context(nc.sbuf_tensor(name, shape, dt))

    ps = [es.enter_context(nc.psum_tensor("ps%d" % i, [128, 512], F32)) for i in range(8)]

    hT = sb("hT", [128, KT, NBLK], BF16)
    mixT = sb("mixT", [128, KT, NBLK], BF16)
    W = [sb("W%d" % i, [128, KT, 128], BF16) for i in range(NWSLOT)]
    xst = sb("xst", [128, D])
    gbc_sb = sb("gbc_sb", [128, D])
    biasP_sb = sb("biasP_sb", [128, NH, 640], BF16)
    biasS_sb = sb("biasS_sb", [128, NH, 160], BF16)
    kbuf = [sb("kbuf%d" % i, [128, 1024], BF16) for i in range(2)]
    vbuf = [sb("vbuf%d" % i, [128, 1024], BF16) for i in range(2)]
    ident_sb = sb("ident_sb", [128, 128])
    ones_bf = sb("ones_bf", [128, 128], BF16)
    cT = sb("cT", [128, KT, 5], BF16)
    gn_sb = sb("gn_sb", [128, KT])
    bada_sb = sb("bada_sb", [128, 96])
    cw_sb = sb("cw_sb", [128, 16, 3])
    cb_sb = sb("cb_sb", [128, 16])
    hb_sb = sb("hb_sb", [128, 2])
    mod = sb("mod", [128, 96, 5])
    gmod = sb("gmod", [128, KT, 5])
    ss8 = sb("ss8", [128, 8])
    ssv = sb("ssv", [128, 4])
    sqj = sb("sqj", [128, 512])
    ucarry = sb("ucarry", [128, 16, 2])
    uout = sb("uout", [128, 16, 8])
    xs_t = [sb("xs_t0", [128, 512])] * 2
    ubuf = [sb("ubuf%d" % i, [128, 544]) for i in range(2)]
    acc = [sb("acc0", [128, 512])] * 2
    a_t = [sb("a_t0", [128, 512])] * 2
    sz = [sb("sz%d" % i, [128, 512]) for i in range(2)]
    qT = [sb("qT%d" % i, [128, 512], BF16) for i in range(2)]
    kf = sb("kf", [128, 512])
    vf = sb("vf", [128, 512])
    kst = sb("kst", [128, 512])
    vst = sb("vst", [128, 512])
    s_sb = [sb("s_sb%d" % i, [128, 640]) for i in range(2)]
    pT = [sb("pT%d" % i, [128, 640], BF16) for i in range(2)]
    rden = sb("rden", [128, 512])
    t1 = sb("t1", [128, 512])
    kc = [sb("kc%d" % i, [128, 512]) for i in range(2)]
    kTc = [sb("kTc%d" % i, [128, 512], BF16) for i in range(2)]
    vc = [sb("vc%d" % i, [128, 512], BF16) for i in range(2)]
    vnew = [sb("vnew%d" % i, [32, 512], BF16) for i in range(2)]
    uoutp = sb("uoutp", [128, 2, 16])
    cc_sb = sb("cc_sb", [8, 128])
    ust = sb("ust", [32, 128])
    yT = ubuf
    xpc = [xs_t[0], acc[0]]
    ypc = [a_t[0], rden]
    xpc_id = [('xs_t', 0), ('acc', 0)]
    ypc_id = [('a_t', 0), 'rden']
    ssq = sb("ssq", [128, 4, 32])
    rstd2 = sb("rstd2", [128, 4])

    st = dict()

    def nxt(k, n):
        v = st.get(k, 0)
        st[k] = (v + 1) % n
        return v

    def ld(key, dst, src, wid, eng='sync'):
        P.dma(eng, key, writes=[wid], out=dst, in_=src)

    ld('c0', ident_sb[:], ident[:], 'ident')
    ld('c1', gn_sb[:], gn[:], 'gn')
    ld('c2', bada_sb[:], bada[:], 'bada')
    ld('c3', cw_sb[:], cw[:], 'cw')
    ld('c4', cb_sb[:], cbias[:], 'cb')
    ld('c5', hb_sb[:], hb[:], 'hb')
    ld('c6', gbc_sb[:], gbc[:], 'gbc')
    ld('c8', cT[:], c5T[:], 'cT', eng='pool')
    ld('c9', ones_bf[:], onesd[:], 'ones', eng='pool')
    ld('c10', biasP_sb[:], biasP[:], 'biasP', eng='pool')
    ld('c11', biasS_sb[:], biasS[:], 'biasS', eng='pool')

    scr_in_parts = [nc.dram_tensor("scr_in%d" % i, [16, 128, KT * 128], BF16, kind="Internal").ap()
                    for i in range(8)]
    scr_out_parts = [nc.dram_tensor("scr_out%d" % i, [16, 128, KT * 128], BF16, kind="Internal").ap()
                     for i in range(2)]

    class _Scr:
        def __init__(self, parts):
            self.parts = parts

        def __getitem__(self, cb):
            return self.parts[cb // 16][cb % 16]

    scr = {id(w_in): _Scr(scr_in_parts), id(w_out): _Scr(scr_out_parts)}
    cached = set()

    def colblock(wsrc, cb, rhs, rhs_ids, n0, N, bank):
        slot = nxt('wslot', NWSLOT)
        ck_ = (id(wsrc), cb)
        if ck_ in cached and st.get('use_cache', True):
            P.dma('pool', ('w', slot), reads=[('scr',) + ck_], writes=[('W', slot)],
                  out=W[slot][:].rearrange("p k c -> p (k c)"), in_=scr[id(wsrc)][cb])
        else:
            P.dma('pool', ('w', slot), writes=[('W', slot)],
                  out=W[slot][:].rearrange("p (a k) c -> p a (k c)", a=2),
                  in_=wsrc[cb].rearrange("p (a f) -> p a f", a=2))
            if id(wsrc) in scr and ck_ not in cached:
                P.dma('sync', ('ws', slot), reads=[('W', slot)], writes=[('scr',) + ck_],
                      out=scr[id(wsrc)][cb], in_=W[slot][:].rearrange("p k c -> p (k c)"))
                cached.add(ck_)
        for kt in range(KT):
            P.op('pe', 'matmul', reads=[('W', slot)] + rhs_ids, writes=[('ps', bank)],
                 out=ps[bank][:, 0:N], lhsT=W[slot][:, kt, :], rhs=rhs[:, kt, n0:n0 + N],
                 start=(kt == 0), stop=(kt == KT - 1))

    def ada_block(cb, bank):
        colblock(w_ada, cb, cT, ['cT'], 0, 5, bank)
        P.op('act', 'activation', reads=[('ps', bank), 'bada'], writes=[('mod', cb)],
             out=mod[:, cb, :], in_=ps[bank][:, 0:5], func=AF.Identity, bias=bada_sb[:, cb:cb + 1], scale=1.0)

    for cb in range(64):
        ada_block(cb, cb % 8)
    for j in range(5):
        P.op('dve', 'scalar_tensor_tensor', reads=[('mod', c) for c in range(32, 64)] + ['gn'],
             writes=[('gmod', j)],
             out=gmod[:, :, j], in0=mod[:, 32:64, j], scalar=1.0, in1=gn_sb[:, :], op0=ALU.add, op1=ALU.mult)

    def norm_front(xsrc, r0):
        P.dma('sync', 'xst', writes=[('xst', 0), ('xst', 1)], out=xst[:], in_=xsrc[r0:r0 + 128, :])
        for c in range(8):
            P.op('act', 'activation', reads=[('xst', c // 4)], writes=[('ss8', c)],
                 out=sqj[:], in_=xst[:, c * 512:(c + 1) * 512], func=AF.Square, accum_out=ss8[:, c:c + 1])
        P.op('dve', 'tensor_reduce', reads=[('ss8', c) for c in range(8)], writes=['ssv0'],
             out=ssv[:, 0:1], in_=ss8[:], axis=AX.X, op=ALU.add)
        P.op('dve', 'tensor_scalar', reads=['ssv0'], writes=['ssv1'],
             out=ssv[:, 1:2], in0=ssv[:, 0:1], scalar1=1.0 / D, scalar2=EPS, op0=ALU.mult, op1=ALU.add)
        P.op('act', 'activation', reads=['ssv1'], writes=['ssv2'],
             out=ssv[:, 2:3], in_=ssv[:, 1:2], func=AF.Sqrt)
        P.op('dve', 'reciprocal', reads=['ssv2'], writes=['ssv3'], out=ssv[:, 3:4], in_=ssv[:, 2:3])
        P.op('act', 'activation', reads=['ssv3', ('xst', 0)], writes=[('xst', 0)],
             out=xst[:, 0:2048], in_=xst[:, 0:2048], func=AF.Identity, scale=ssv[:, 3:4], bias=0.0)
        P.op('dve', 'tensor_scalar', reads=['ssv3', ('xst', 1)], writes=[('xst', 1)],
             out=xst[:, 2048:4096], in0=xst[:, 2048:4096], scalar1=ssv[:, 3:4], scalar2=None, op0=ALU.mult)

    def norm_back(tt, sample):
        for k4 in range(8):
            bank = nxt('tb', 8)
            for j in range(4):
                kt = k4 * 4 + j
                P.op('pe', 'transpose', reads=[('xst', kt // 16), 'ident'], writes=[('ps', bank)],
                     out=ps[bank][:, j * 128:(j + 1) * 128], in_=xst[:, kt * 128:(kt + 1) * 128],
                     identity=ident_sb[:])
            for j in range(4):
                kt = k4 * 4 + j
                segs = [(0, 128, 0)] if not sample else [(b * 32, 32, 1 + b) for b in range(4)]
                for (c0, cn, mj) in segs:
                    o = hT[:, kt, tt * 128 + c0: tt * 128 + c0 + cn]
                    i = ps[bank][:, j * 128 + c0: j * 128 + c0 + cn]
                    rd = [('ps', bank), ('gmod', mj), ('mod', kt)]
                    P.op('dve', 'tensor_scalar', reads=rd, writes=[('hT', tt, kt)],
                         out=o, in0=i, scalar1=gmod[:, kt, mj:mj + 1], scalar2=mod[:, kt, mj:mj + 1],
                         op0=ALU.mult, op1=ALU.add)

    def norm_stage(xsrc, row0, ntiles, sample):
        for tt in range(ntiles):
            norm_front(xsrc, row0 + tt * 128)
            norm_back(tt, sample)

    def norm_hooks(xsrc, row0, ntiles, sample):
        hooks = {}
        step = 32 // ntiles
        for tt in range(ntiles):
            hooks.setdefault(tt * step, []).append(
                (lambda tt: lambda: norm_front(xsrc, row0 + tt * 128))(tt))
            hooks.setdefault(tt * step + min(3, step - 1), []).append(
                (lambda tt: lambda: norm_back(tt, sample))(tt))
        return hooks

    def hT_ids(ntiles):
        return [('hT', tt, kt) for tt in range(ntiles) for kt in range(KT)]

    def conv_group(g, N, ntiles, kind):
        b0 = 4 * (g % 2)
        i2 = g % 2
        hid = hT_ids(ntiles)
        sample = (kind == 'sample')
        ub = ubuf[i2]
        if kind == 'halo':
            colblock(w_in, g, hT, hid, 480, 32, b0)
            colblock(w_in, 32 + g, hT, hid, 480, 32, b0 + 2)
            P.op('act', 'activation', reads=[('ps', b0)], writes=[('xs_t', 0)],
                 out=xs_t[i2][:, 0:32], in_=ps[b0][:, 0:32], func=AF.Copy)
            P.op('dve', 'tensor_tensor', reads=[('ps', b0 + 2), ('xs_t', 0)], writes=[('ubuf', i2)],
                 out=ub[:, 0:32], in0=ps[b0 + 2][:, 0:32], in1=xs_t[i2][:, 0:32], op=ALU.mult)
            P.op('dve', 'tensor_scalar', reads=[('ubuf', i2), 'hb'], writes=[('ucarry', g)],
                 out=ucarry[:, g, :], in0=ub[:, 30:32], scalar1=hb_sb[:, 1:2], scalar2=None, op0=ALU.mult)
            return
        if sample:
            ub3 = ub[:, 0:136].rearrange("p (b t) -> p b t", t=34)
            P.dma('sync', 'cc', writes=['cc'], out=cc_sb[:], in_=cc[:, g * 128:(g + 1) * 128])
            P.op('pe', 'transpose', reads=['cc', 'ident'], writes=[('ps', b0 + 3)],
                 out=ps[b0 + 3][:, 504:512], in_=cc_sb[0:8, :], identity=ident_sb[0:8, 0:8])
            P.op('act', 'activation', reads=[('ps', b0 + 3)], writes=[('ubuf', i2)],
                 out=ub3[:, :, 0:2], in_=ps[b0 + 3][:, 504:512].rearrange("p (b t) -> p b t", t=2),
                 func=AF.Copy)
        colblock(w_in, g, hT, hid, 0, N, b0)
        colblock(w_in, 32 + g, hT, hid, 0, N, b0 + 2)
        colblock(w_in, 16 + g, hT, hid, 0, N, b0 + 1)
        colblock(w_in, 48 + g, hT, hid, 0, N, b0 + 3)
        if not sample:
            uview = lambda off: ub[:, off:off + N]
            flat = lambda t: t[:, 0:N]
            P.op('act', 'activation', reads=[('ucarry', g)], writes=[('ubuf', i2)],
                 out=ub[:, 0:2], in_=ucarry[:, g, :], func=AF.Copy)
        else:
            ub3 = ub[:, 0:136].rearrange("p (b t) -> p b t", t=34)
            uview = lambda off: ub3[:, :, off:off + 32]
            flat = lambda t: t[:, 0:N].rearrange("p (b t) -> p b t", t=32)
        P.op('act', 'activation', reads=[('ps', b0)], writes=[('xs_t', 0)],
             out=xs_t[i2][:, 0:N], in_=ps[b0][:, 0:N], func=AF.Copy)
        P.op('dve', 'tensor_tensor', reads=[('ps', b0 + 2), ('xs_t', 0), ('ubuf', i2)], writes=[('ubuf', i2)],
             out=uview(2), in0=flat(ps[b0 + 2]), in1=flat(xs_t[i2]), op=ALU.mult)
        P.op('dve', 'tensor_scalar', reads=[('ubuf', i2), 'cw', 'cb'], writes=[('acc', 0)],
             out=flat(acc[i2]), in0=uview(0), scalar1=cw_sb[:, g, 0:1], scalar2=cb_sb[:, g:g + 1],
             op0=ALU.mult, op1=ALU.add)
        for tap in (1, 2):
            P.op('dve', 'scalar_tensor_tensor', reads=[('ubuf', i2), 'cw', ('acc', 0)], writes=[('acc', 0)],
                 out=flat(acc[i2]), in0=uview(tap), scalar=cw_sb[:, g, tap:tap + 1], in1=flat(acc[i2]),
                 op0=ALU.mult, op1=ALU.add)
        P.op('dve', 'tensor_tensor', reads=[('ps', b0 + 1), ('acc', 0)], writes=[('a_t', 0)],
             out=a_t[i2][:, 0:N], in0=ps[b0 + 1][:, 0:N], in1=acc[i2][:, 0:N], op=ALU.mult)
        P.op('act', 'activation', reads=[('ps', b0 + 3)], writes=[('sz', i2)],
             out=sz[i2][:, 0:N], in_=ps[b0 + 3][:, 0:N], func=AF.Silu)
        P.op('dve', 'tensor_tensor', reads=[('a_t', 0), ('sz', i2)], writes=[('mixT', g)],
             out=mixT[:, g, 0:N], in0=a_t[i2][:, 0:N], in1=sz[i2][:, 0:N], op=ALU.mult)
        if not sample:
            P.op('act', 'activation', reads=[('ubuf', i2)], writes=[('ucarry', g)],
                 out=ucarry[:, g, :], in_=ub[:, N:N + 2], func=AF.Copy)
        else:
            P.op('act', 'activation', reads=[('ubuf', i2)], writes=[('uout', g)],
                 out=uout[:, g, :].rearrange("p (b t) -> p b t", t=2), in_=ub3[:, :, 32:34], func=AF.Copy)

    def head_proj_steps(h, N, ntiles, kind, pidx):
        hid = hT_ids(ntiles)
        i2 = h % 2
        sample = (kind == 'sample')
        kb, vb = kbuf[i2], vbuf[i2]
        out_pass = sample or (kind == 'prompt' and pidx == NPASS - 1)
        seg = 32 if sample else 128
        nseg = N // seg

        def step_q():
            if kind == 'prompt':
                P.dma('sync', ('kprev', i2), reads=[('kscr', h)], writes=[('kb_prev', i2)],
                      out=kb[:, 0:512], in_=kscr[h])
                P.dma('sync', ('vprev', i2), reads=[('vscr', h)], writes=[('vb_prev', i2)],
                      out=vb[:, 0:512], in_=vscr[h])
            if kind != 'halo':
                colblock(w_in, 64 + h, hT, hid, 0, N, 0)
                P.op('act', 'activation', reads=[('ps', 0)], writes=[('qT', i2)],
                     out=qT[i2][:, 0:N], in_=ps[0][:, 0:N], func=AF.Copy, scale=ATTN_SCALE)

        def step_k():
            colblock(w_in, 80 + h, hT, hid, 0, N, 1)
            if out_pass:
                P.op('dve', 'tensor_copy', reads=[('ps', 1)], writes=['kf'], out=kf[:, 0:N], in_=ps[1][:, 0:N])
                P.op('act', 'activation', reads=['kf'], writes=[('kb_cur', i2)],
                     out=kb[:, 512:512 + N], in_=kf[:, 0:N], func=AF.Copy)
            else:
                P.op('act', 'activation', reads=[('ps', 1)], writes=[('kb_cur', i2)],
                     out=kb[:, 512:512 + N], in_=ps[1][:, 0:N], func=AF.Copy)

        def step_v():
            colblock(w_in, 96 + h, hT, hid, 0, N, 2)
            P.op('dve', 'tensor_copy', reads=[('ps', 2)], writes=['vf'], out=vf[:, 0:N], in_=ps[2][:, 0:N])

        def step_z():
            if kind != 'halo':
                colblock(w_in, 112 + h, hT, hid, 0, N, 3)
                P.op('act', 'activation', reads=[('ps', 3)], writes=[('sz', i2)],
                     out=sz[i2][:, 0:N], in_=ps[3][:, 0:N], func=AF.Silu)

        def step_post():
            for s in range(nseg):
                P.op('pe', 'transpose', reads=['vf', 'ident'], writes=[('ps', 4)],
                     out=ps[4][0:seg, s * 128:(s + 1) * 128], in_=vf[:, s * seg:(s + 1) * seg],
                     identity=ident_sb[:])
            vsrc, vsrc_id = ps[4], ('ps', 4)
            if out_pass:
                P.op('dve', 'tensor_copy', reads=[('ps', 4)], writes=['vst'],
                     out=vst[0:seg, :], in_=ps[4][0:seg, 0:512])
                vsrc, vsrc_id = vst, 'vst'
            if not sample:
                P.op('act', 'activation', reads=[vsrc_id], writes=[('vb_cur', i2)],
                     out=vb[:, 512:1024], in_=vsrc[:, 0:512], func=AF.Copy)
            else:
                P.op('act', 'activation', reads=[vsrc_id], writes=[('vnew', i2)],
                     out=vnew[i2][0:32, :], in_=vsrc[0:32, 0:512], func=AF.Copy)
            if out_pass:
                vdst = vsn if sample else vp
                kdst = ksn if sample else kp
                pat = "(b t) d -> t b d" if sample else "(tt p) d -> p tt d"
                kw = dict(t=32) if sample else dict(p=128)
                P.dma('sync', 'vst', reads=['vst'],
                      out=vdst[:, h * 128:(h + 1) * 128].rearrange(pat, **kw),
                      in_=vst[0:seg, :].rearrange("p (a d) -> p a d", d=128))
                for s in range(nseg):
                    P.op('pe', 'transpose', reads=['kf', 'ident'], writes=[('ps', 5)],
                         out=ps[5][0:seg, s * 128:(s + 1) * 128], in_=kf[:, s * seg:(s + 1) * seg],
                         identity=ident_sb[:])
                P.op('act', 'activation', reads=[('ps', 5)], writes=['kst'],
                     out=kst[0:seg, :], in_=ps[5][0:seg, 0:512], func=AF.Copy)
                P.dma('sync', 'kst', reads=['kst'],
                      out=kdst[:, h * 128:(h + 1) * 128].rearrange(pat, **kw),
                      in_=kst[0:seg, :].rearrange("p (a d) -> p a d", d=128))
            if kind == 'halo' or (kind == 'prompt' and pidx < NPASS - 1):
                P.dma('sync', ('kcur', i2), reads=[('kb_cur', i2)], writes=[('kscr', h)],
                      out=kscr[h], in_=kb[:, 512:1024])
                P.dma('sync', ('vcur', i2), reads=[('vb_cur', i2)], writes=[('vscr', h)],
                      out=vscr[h], in_=vb[:, 512:1024])

        return [step_q, step_k, step_v, step_z, step_post]

    def head_proj(h, N, ntiles, kind, pidx):
        for s in head_proj_steps(h, N, ntiles, kind, pidx):
            s()

    def attn_epilogue(h, N):
        i2 = h % 2
        P.op('dve', 'reciprocal', reads=[('ps', 7)], writes=['rden'], out=rden[:, 0:N], in_=ps[7][:, 0:N])
        P.op('dve', 'tensor_tensor', reads=[('ps', 6), 'rden'], writes=['t1'],
             out=t1[:, 0:N], in0=ps[6][:, 0:N], in1=rden[:, 0:N], op=ALU.mult)
        P.op('dve', 'tensor_tensor', reads=['t1', ('sz', i2)], writes=[('mixT', 16 + h)],
             out=mixT[:, 16 + h, 0:N], in0=t1[:, 0:N], in1=sz[i2][:, 0:N], op=ALU.mult)

    def head_attn_steps(h, pidx):
        i2 = h % 2
        kb, vb = kbuf[i2], vbuf[i2]
        kv_ids = [('kb_prev', i2), ('kb_cur', i2)]
        vv_ids = [('vb_prev', i2), ('vb_cur', i2)]
        sis = {}

        def qk(j):
            si = nxt('si', 2)
            sis[j] = si
            for t in range(5):
                bank, col = (4, t * 128) if t < 4 else (5, 0)
                P.op('pe', 'matmul', reads=kv_ids + [('qT', i2)], writes=[('ps', bank)],
                     out=ps[bank][:, col:col + 128], lhsT=kb[:, (j + t) * 128:(j + t + 1) * 128],
                     rhs=qT[i2][:, j * 128:(j + 1) * 128], start=True, stop=True)
            nhalo = max(0, 4 - j) if pidx == 0 else 0
            segs = []
            if nhalo > 0:
                segs.append((0, min(nhalo, 4), True))
            if nhalo < 4:
                segs.append((nhalo, 4, False))
            segs.append((4, 5, False))
            for (t0, t1_, halo) in segs:
                bank, col = (4, t0 * 128) if t0 < 4 else (5, 0)
                n = (t1_ - t0) * 128
                o = s_sb[si][:, t0 * 128:t0 * 128 + n]
                i = ps[bank][:, col:col + n]
                bi = biasP_sb[:, h, t0 * 128:t0 * 128 + n]
                if halo:
                    P.op('dve', 'scalar_tensor_tensor', reads=[('ps', bank), 'biasP', 'hb'], writes=[('s_sb', si)],
                         out=o, in0=i, scalar=hb_sb[:, 0:1], in1=bi, op0=ALU.add, op1=ALU.add)
                else:
                    P.op('dve', 'tensor_tensor', reads=[('ps', bank), 'biasP'], writes=[('s_sb', si)],
                         out=o, in0=i, in1=bi, op=ALU.add)
            P.op('act', 'activation', reads=[('s_sb', si)], writes=[('pT', si)],
                 out=pT[si][:], in_=s_sb[si][:], func=AF.Exp)

        def pv(j):
            si = sis[j]
            for t in range(5):
                P.op('pe', 'matmul', reads=vv_ids + [('pT', si)], writes=[('ps', 6)],
                     out=ps[6][:, j * 128:(j + 1) * 128], lhsT=vb[:, (j + t) * 128:(j + t + 1) * 128],
                     rhs=pT[si][:, t * 128:(t + 1) * 128], start=(t == 0), stop=(t == 4))
                P.op('pe', 'matmul', reads=['ones', ('pT', si)], writes=[('ps', 7)],
                     out=ps[7][:, j * 128:(j + 1) * 128], lhsT=ones_bf[:],
                     rhs=pT[si][:, t * 128:(t + 1) * 128], start=(t == 0), stop=(t == 4))

        return qk, pv

    def heads_prompt(pidx):
        head_proj(0, NBLK, 4, 'prompt', pidx)
        for h in range(NH):
            nx = head_proj_steps(h + 1, NBLK, 4, 'prompt', pidx) if h + 1 < NH else [lambda: None] * 5
            qk, pv = head_attn_steps(h, pidx)
            nx[0]()
            qk(0)
            nx[1]()
            pv(0)
            qk(1)
            nx[2]()
            pv(1)
            qk(2)
            nx[3]()
            pv(2)
            qk(3)
            nx[4]()
            pv(3)
            attn_epilogue(h, NBLK)

    def head_attn_sample(h):
        i2 = h % 2
        kb = kbuf[i2]
        for b in range(4):
            ci = nxt('ci', 2)
            si = nxt('si', 2)
            P.dma('sync', ('kc', ci), writes=[('kc', ci)],
                  out=kc[ci][:].rearrange("p (t d) -> p t d", d=128),
                  in_=ck[b, :, h, :].rearrange("(t p) d -> p t d", p=128))
            P.dma('pool', ('vc', ci), writes=[('vc', ci)],
                  out=vc[ci][:].rearrange("p (t d) -> p t d", d=128),
                  in_=cv[b, :, h, :].rearrange("(t p) d -> p t d", p=128))
            for t in range(4):
                P.op('pe', 'transpose', reads=[('kc', ci), 'ident'], writes=[('ps', 5)],
                     out=ps[5][:, t * 128:(t + 1) * 128], in_=kc[ci][:, t * 128:(t + 1) * 128],
                     identity=ident_sb[:])
            P.op('act', 'activation', reads=[('ps', 5)], writes=[('kTc', ci)],
                 out=kTc[ci][:], in_=ps[5][:, 0:512], func=AF.Copy)
            qs = qT[i2][:, b * 32:(b + 1) * 32]
            for t in range(4):
                P.op('pe', 'matmul', reads=[('kTc', ci), ('qT', i2)], writes=[('ps', 4)],
                     out=ps[4][:, t * 32:(t + 1) * 32], lhsT=kTc[ci][:, t * 128:(t + 1) * 128], rhs=qs,
                     start=True, stop=True)
            P.op('pe', 'matmul', reads=[('kb_cur', i2), ('qT', i2)], writes=[('ps', 4)],
                 out=ps[4][0:32, 128:160], lhsT=kb[:, 512 + b * 32:512 + (b + 1) * 32], rhs=qs,
                 start=True, stop=True)
            P.op('dve', 'tensor_tensor', reads=[('ps', 4), 'biasS'], writes=[('s_sb', si)],
                 out=s_sb[si][:, 0:128], in0=ps[4][:, 0:128], in1=biasS_sb[:, h, 0:128], op=ALU.add)
            P.op('dve', 'tensor_tensor', reads=[('ps', 4), 'biasS'], writes=[('s_sb', si)],
                 out=s_sb[si][0:32, 128:160], in0=ps[4][0:32, 128:160], in1=biasS_sb[0:32, h, 128:160], op=ALU.add)
            P.op('act', 'activation', reads=[('s_sb', si)], writes=[('pT', si)],
                 out=pT[si][:, 0:128], in_=s_sb[si][:, 0:128], func=AF.Exp)
            P.op('act', 'activation', reads=[('s_sb', si)], writes=[('pT', si)],
                 out=pT[si][0:32, 128:160], in_=s_sb[si][0:32, 128:160], func=AF.Exp)
            for t in range(5):
                if t < 4:
                    lv = vc[ci][:, t * 128:(t + 1) * 128]
                    lo = ones_bf[:]
                    r = pT[si][:, t * 32:(t + 1) * 32]
                else:
                    lv = vnew[i2][0:32, b * 128:(b + 1) * 128]
                    lo = ones_bf[0:32, :]
                    r = pT[si][0:32, 128:160]
                P.op('pe', 'matmul', reads=[('vc', ci), ('vnew', i2), ('pT', si)], writes=[('ps', 6)],
                     out=ps[6][:, b * 32:(b + 1) * 32], lhsT=lv, rhs=r, start=(t == 0), stop=(t == 4))
                P.op('pe', 'matmul', reads=['ones', ('pT', si)], writes=[('ps', 7)],
                     out=ps[7][:, b * 32:(b + 1) * 32], lhsT=lo, rhs=r, start=(t == 0), stop=(t == 4))
        attn_epilogue(h, TS)

    def out_proj(N, ntiles, xsrc, xrow0, ydst, yrow0, sample, hooks=None):
        mids = [('mixT', k) for k in range(KT)]
        for cb in range(KT):
            for hk in (hooks or {}).get(cb, []):
                hk()
            bank = cb % 4
            yi = nxt('yi', 2)
            colblock(w_out, cb, mixT, mids, 0, N, bank)
            segs = [(0, N, 0)] if not sample else [(b * 32, 32, 1 + b) for b in range(4)]
            for (c0, cn, mj) in segs:
                P.op('act', 'activation', reads=[('ps', bank), ('mod', 64 + cb)], writes=[('ubuf', yi)],
                     out=yT[yi][:, c0:c0 + cn], in_=ps[bank][:, c0:c0 + cn], func=AF.Identity,
                     scale=mod[:, 64 + cb, mj:mj + 1], bias=0.0)
            tbk = 4 + (cb % 4)
            for tt in range(ntiles):
                P.op('pe', 'transpose', reads=[('ubuf', yi), 'ident'], writes=[('ps', tbk)],
                     out=ps[tbk][:, tt * 128:(tt + 1) * 128], in_=yT[yi][:, tt * 128:(tt + 1) * 128],
                     identity=ident_sb[:])
            P.dma('sync', ('xpc', yi), writes=[xpc_id[yi]],
                  out=xpc[yi][:, 0:N].rearrange("p (a c) -> p a c", c=128),
                  in_=xsrc[xrow0:xrow0 + N, cb * 128:(cb + 1) * 128].rearrange("(a p) c -> p a c", p=128))
            P.op('dve', 'tensor_tensor', reads=[('ps', tbk), xpc_id[yi]], writes=[ypc_id[yi]],
                 out=ypc[yi][:, 0:N], in0=ps[tbk][:, 0:N], in1=xpc[yi][:, 0:N], op=ALU.add)
            for tt in range(ntiles):
                P.op('act', 'activation', reads=[ypc_id[yi]], writes=[('ssq', tt, cb)],
                     out=sqj[:, 0:128], in_=ypc[yi][:, tt * 128:(tt + 1) * 128], func=AF.Square,
                     accum_out=ssq[:, tt, cb:cb + 1])
            P.dma('sync', ('ypc', yi), reads=[ypc_id[yi]], writes=[('ydram', cb)],
                  out=ydst[yrow0:yrow0 + N, cb * 128:(cb + 1) * 128].rearrange("(a p) c -> p a c", p=128),
                  in_=ypc[yi][:, 0:N].rearrange("p (a c) -> p a c", c=128))
        for tt in range(ntiles):
            r2 = rstd2[:, tt:tt + 1]
            P.op('dve', 'tensor_reduce', reads=[('ssq', tt, c) for c in range(KT)], writes=[('r2a', tt)],
                 out=r2, in_=ssq[:, tt, :], axis=AX.X, op=ALU.add)
            P.op('dve', 'tensor_scalar', reads=[('r2a', tt)], writes=[('r2a', tt)],
                 out=r2, in0=r2, scalar1=1.0 / D, scalar2=EPS, op0=ALU.mult, op1=ALU.add)
            P.op('act', 'activation', reads=[('r2a', tt)], writes=[('r2a', tt)], out=r2, in_=r2, func=AF.Sqrt)
            P.op('dve', 'reciprocal', reads=[('r2a', tt)], writes=[('r2a', tt)], out=r2, in_=r2)
            r0 = yrow0 + tt * 128
            P.dma('sync', 'xst', reads=[('ydram', c) for c in range(KT)], writes=[('xst', 0), ('xst', 1)],
                  out=xst[:], in_=ydst[r0:r0 + 128, :])
            P.op('dve', 'scalar_tensor_tensor', reads=[('xst', 0), ('xst', 1), ('r2a', tt), 'gbc'],
                 writes=[('xst', 0), ('xst', 1)],
                 out=xst[:], in0=xst[:], scalar=r2, in1=gbc_sb[:], op0=ALU.mult, op1=ALU.mult)
            P.dma('sync', 'yfin', reads=[('xst', 0), ('xst', 1)], writes=[('ydram', c) for c in range(KT)],
                  out=ydst[r0:r0 + 128, :], in_=xst[:])

    def program():
        norm_stage(xp, 0, 4, False)
        for g in range(16):
            conv_group(g, NBLK, 4, 'halo')
        for h in range(NH):
            head_proj(h, NBLK, 4, 'halo', -1)
        norm_stage(xp, HALO, 4, False)
        for pidx in range(NPASS):
            row0 = HALO + pidx * NBLK
            for g in range(16):
                if pidx == 0:
                    ada_block(64 + 2 * g, 4 * (g % 2))
                    ada_block(64 + 2 * g + 1, 4 * (g % 2) + 1)
                conv_group(g, NBLK, 4, 'prompt')
            heads_prompt(pidx)
            if pidx + 1 < NPASS:
                hooks = norm_hooks(xp, row0 + NBLK, 4, False)
            else:
                hooks = norm_hooks(xs, 0, 1, True)
            out_proj(NBLK, 4, xp, row0, yp, pidx * NBLK, False, hooks)
            if pidx == NPASS - 1:
                for g in range(16):
                    P.op('act', 'activation', reads=[('ucarry', g)], writes=[('uoutp', g)],
                         out=uoutp[:, :, g], in_=ucarry[:, g, :], func=AF.Copy)
                P.op('pe', 'transpose', reads=[('uoutp', g) for g in range(16)] + ['ident'], writes=[('ps', 0)],
                     out=ps[0][0:32, 0:128], in_=uoutp[:, :, :], identity=ident_sb[:])
                P.op('act', 'activation', reads=[('ps', 0)], writes=['ust'],
                     out=ust[0:32, 0:128], in_=ps[0][0:32, 0:128], func=AF.Copy)
                for t in range(2):
                    P.dma('sync', 'ust', reads=['ust'],
                          out=up[t, :].rearrange("(g p) -> g p", p=128), in_=ust[t * 16:(t + 1) * 16, 0:128])
        for g in range(16):
            conv_group(g, TS, 1, 'sample')
        head_proj(0, TS, 1, 'sample', -1)
        for h in range(NH):
            if h + 1 < NH:
                head_proj(h + 1, TS, 1, 'sample', -1)
            head_attn_sample(h)
        out_proj(TS, 1, xs, 0, ys, 0, True)
        for g4 in range(4):
            P.op('pe', 'transpose', reads=[('uout', g) for g in range(16)] + ['ident'], writes=[('ps', 0)],
                 out=ps[0][0:32, g4 * 128:(g4 + 1) * 128], in_=uout[:, g4 * 4:(g4 + 1) * 4, :],
                 identity=ident_sb[:])
        P.op('act', 'activation', reads=[('ps', 0)], writes=['kst'],
             out=kst[0:32, 0:512], in_=ps[0][0:32, 0:512], func=AF.Copy)
        for g in range(16):
            g4, gl = g // 4, g % 4
            P.dma('sync', 'kst', reads=['kst'],
                  out=usn[:, g * 128:(g + 1) * 128], in_=kst[gl * 8:(gl + 1) * 8, g4 * 128:(g4 + 1) * 128])

    program()
    P.emit(nc, es)
    es.close()
    return nc


def _bias_tiles(rel_bias):
    rb = rel_bias[0]
    p = np.arange(128)[:, None, None]
    t = np.arange(5)[None, :, None]
    q = np.arange(128)[None, None, :]
    rel = q - ((t - 4) * 128 + p)
    idx = np.clip(rel, -256, 256) + 256
    bp = rb[:, idx]
    qc = q // 64
    kc = ((t - 4) * 128 + p) // 64
    visible = (kc <= qc) & (kc >= qc - 8)
    bp = np.where(visible[None], bp, np.float32(NEG)).astype(np.float32)
    biasP = np.ascontiguousarray(bp.transpose(1, 0, 2, 3)).reshape(128, 16, 640)
    j = (np.arange(5)[None, :, None] * 128 + np.arange(128)[:, None, None])
    tq = np.arange(32)[None, None, :]
    rel_s = 512 + tq - j
    idx_s = np.clip(rel_s, -256, 256) + 256
    bs = rb[:, idx_s]
    bs = np.where((j < 544)[None], bs, np.float32(NEG)).astype(np.float32)
    biasS = np.ascontiguousarray(bs.transpose(1, 0, 2, 3)).reshape(128, 16, 160)
    return biasP, biasS


_NC_CACHE = {}


def kernel(x_prompt, x_sample, cache_k, cache_v, cache_conv, c_prompt, c_sample,
           g_norm, w_ada, b_ada, w_in, conv_w, conv_b, rel_bias, w_out, g_final):
    f32 = np.float32
    x_prompt = np.asarray(x_prompt, f32); x_sample = np.asarray(x_sample, f32)
    cache_k = np.asarray(cache_k, f32); cache_v = np.asarray(cache_v, f32)
    cache_conv = np.asarray(cache_conv, f32)
    def blk(w):
        n = w.shape[1] // 128
        return np.ascontiguousarray(w.reshape(KT, 128, n, 128).transpose(2, 1, 0, 3)).reshape(n, 128, KT * 128)
    w_ada2 = blk(np.asarray(w_ada, f32)[0])
    w_in2 = blk(np.asarray(w_in, f32)[0])
    w_out2 = blk(np.asarray(w_out, f32)[0])
    fm = lambda v, n: np.ascontiguousarray(np.asarray(v, f32).reshape(n, 128).T)
    gn = fm(g_norm[0], 32)
    bada = fm(b_ada[0], 96)
    cw = np.ascontiguousarray(np.asarray(conv_w, f32)[0].reshape(3, 16, 128).transpose(2, 1, 0))
    cbias = fm(conv_b[0], 16)
    gbc = np.ascontiguousarray(np.broadcast_to(np.asarray(g_final, f32)[None, :], (128, D)))
    biasP, biasS = _bias_tiles(np.asarray(rel_bias, f32))
    ident = np.eye(128, dtype=f32)
    onesd = np.ones((128, 128), f32)
    xfull = np.concatenate([np.zeros((HALO, D), f32), x_prompt[0]], axis=0)

    in_maps = []
    for c in range(NCORES):
        c5 = np.concatenate([np.asarray(c_prompt, f32), np.asarray(c_sample, f32)[4 * c:4 * c + 4]], axis=0)
        c5T = np.ascontiguousarray(c5.T.reshape(32, 128, 5).transpose(1, 0, 2))
        hbv = np.zeros((128, 2), f32)
        hbv[:, 0] = NEG if c == 0 else 0.0
        hbv[:, 1] = 0.0 if c == 0 else 1.0
        in_maps.append(dict(
            xp=np.ascontiguousarray(xfull[c * TP: c * TP + HALO + TP]),
            xs=np.ascontiguousarray(x_sample[4 * c:4 * c + 4].reshape(TS, D)),
            ck=np.ascontiguousarray(cache_k[0, 4 * c:4 * c + 4]),
            cv=np.ascontiguousarray(cache_v[0, 4 * c:4 * c + 4]),
            cc=np.ascontiguousarray(cache_conv[0, 4 * c:4 * c + 4].reshape(8, 2048)),
            c5T=c5T, gn=gn, bada=bada, cw=cw, cbias=cbias, gbc=gbc,
            w_ada=w_ada2, w_in=w_in2, w_out=w_out2, biasP=biasP, biasS=biasS, hb=hbv,
            ident=ident, onesd=onesd))
    if 'nc' not in _NC_CACHE:
        _NC_CACHE['nc'] = build_nc()
    res = run_bass_kernel_spmd(_NC_CACHE['nc'], in_maps, core_ids=list(range(NCORES)))
    R = res.results
    y_prompt = np.concatenate([R[c]["yp"] for c in range(NCORES)], axis=0)[None]
    y_sample = np.concatenate([R[c]["ys"].reshape(4, 32, D) for c in range(NCORES)], axis=0)
    new_k_prompt = R[NCORES - 1]["kp"].reshape(1, 1, 512, NH, 128)
    new_v_prompt = R[NCORES - 1]["vp"].reshape(1, 1, 512, NH, 128)
    new_conv_prompt = R[NCORES - 1]["up"].reshape(1, 1, 2, 2048)
    new_k_sample = np.concatenate([R[c]["ksn"].reshape(4, 32, NH, 128) for c in range(NCORES)], axis=0)[None]
    new_v_sample = np.concatenate([R[c]["vsn"].reshape(4, 32, NH, 128) for c in range(NCORES)], axis=0)[None]
    new_conv_sample = np.concatenate([R[c]["usn"].reshape(4, 2, 2048) for c in range(NCORES)], axis=0)[None]
    return tuple(np.ascontiguousarray(a, dtype=f32) for a in (
        y_prompt, y_sample, new_k_prompt, new_v_prompt, new_conv_prompt,
        new_k_sample, new_v_sample, new_conv_sample))
```

```python
import numpy as np
from contextlib import ExitStack
import concourse.bass as bass
import concourse.mybir as mybir
from concourse.bass_utils import run_bass_kernel_spmd

F32 = mybir.dt.float32
BF16 = mybir.dt.bfloat16
AF = mybir.ActivationFunctionType
ALU = mybir.AluOpType
AX = mybir.AxisListType

NCORES = 8
D = 4096
KT = 32
TP = 2048
HALO = 512
NBLK = 512
NPASS = TP // NBLK
TS = 128
NH = 16
EPS = 1e-6
ATTN_SCALE = 128 ** -0.5
NEG = -30000.0
SAME_ENGINE_SYNC = True
NWSLOT = 3

ENGS = ['sync', 'act', 'dve', 'pool', 'pe']


class Prog:
    def __init__(self):
        self.ops = {e: [] for e in ENGS}
        self.bufs = {}
        self.dma_count = {}

    def _deps(self, reads, writes, tok):
        deps = []
        for b in reads:
            st = self.bufs.setdefault(b, [None, []])
            if st[0] is not None:
                deps.append(st[0])
        for b in writes:
            st = self.bufs.setdefault(b, [None, []])
            if st[0] is not None:
                deps.append(st[0])
            deps.extend(st[1])
        for b in reads:
            self.bufs[b][1].append(tok)
        for b in writes:
            self.bufs[b] = [tok, []]
        return deps

    def op(self, eng, name, reads=(), writes=(), **kw):
        idx = len(self.ops[eng])
        tok = ('c', eng, idx)
        deps = self._deps(list(reads), list(writes), tok)
        fn = (lambda name, kw: lambda e: getattr(e, name)(**kw))(name, kw)
        self.ops[eng].append(dict(fn=fn, deps=deps, sig=False, dma=None))

    def dma(self, eng, key, reads=(), writes=(), **kw):
        cnt = self.dma_count.get(key, 0) + 16
        self.dma_count[key] = cnt
        tok = ('d', key, cnt)
        deps = self._deps(list(reads), list(writes), tok)
        fn = (lambda kw: lambda e: e.dma_start(**kw))(kw)
        self.ops[eng].append(dict(fn=fn, deps=deps, sig=False, dma=(key, cnt)))

    def emit(self, nc, es):
        sems = {e: es.enter_context(nc.semaphore('s_' + e)) for e in ENGS}
        dsems = {k: es.enter_context(nc.semaphore('d%d' % i)) for i, k in enumerate(self.dma_count)}
        for e in ENGS:
            for o in self.ops[e]:
                best = {}
                for d in o['deps']:
                    if d[0] == 'c':
                        if d[1] == e and (e == 'pe' or not SAME_ENGINE_SYNC):
                            continue
                        k = ('c', d[1])
                        if k not in best or best[k][2] < d[2]:
                            best[k] = d
                    else:
                        k = ('d', d[1])
                        if k not in best or best[k][2] < d[2]:
                            best[k] = d
                for d in best.values():
                    if d[0] == 'c':
                        self.ops[d[1]][d[2]]['sig'] = True
                o['deps'] = list(best.values())
        for e in ENGS:
            c = 0
            for o in self.ops[e]:
                if o['sig']:
                    c += 1
                o['sv'] = c
        final = [('d', k, v) for k, v in self.dma_count.items()]
        block = es.enter_context(nc.Block())

        def run(ename, eng):
            waited = {}
            for o in self.ops[ename]:
                for d in o['deps']:
                    if d[0] == 'c':
                        key, sem, val = ('c', d[1]), sems[d[1]], self.ops[d[1]][d[2]]['sv']
                    else:
                        key, sem, val = ('d', d[1]), dsems[d[1]], d[2]
                    if waited.get(key, 0) < val:
                        eng.wait_ge(sem, val)
                        waited[key] = val
                ins = o['fn'](eng)
                if o['dma'] is not None:
                    ins.then_inc(dsems[o['dma'][0]], 16)
                elif o['sig']:
                    ins.then_inc(sems[ename], 1)
            if ename == 'sync':
                for d in final:
                    eng.wait_ge(dsems[d[1]], d[2])

        @block.sync
        def _(eng):
            run('sync', eng)

        @block.scalar
        def _(eng):
            run('act', eng)

        @block.vector
        def _(eng):
            run('dve', eng)

        @block.gpsimd
        def _(eng):
            run('pool', eng)

        @block.tensor
        def _(eng):
            run('pe', eng)


def build_nc():
    nc = bass.Bass("TRN2", target_bir_lowering=False)
    P = Prog()
    es = ExitStack()

    def din(name, shape):
        return nc.dram_tensor(name, shape, F32, kind="ExternalInput").ap()

    def dout(name, shape):
        return nc.dram_tensor(name, shape, F32, kind="ExternalOutput").ap()

    xp = din("xp", [HALO + TP, D])
    xs = din("xs", [TS, D])
    ck = din("ck", [4, 512, NH, 128])
    cv = din("cv", [4, 512, NH, 128])
    cc = din("cc", [8, 2048])
    c5T = din("c5T", [128, KT, 5])
    gn = din("gn", [128, KT])
    bada = din("bada", [128, 96])
    cw = din("cw", [128, 16, 3])
    cbias = din("cbias", [128, 16])
    gbc = din("gbc", [128, D])
    w_ada = din("w_ada", [96, 128, KT * 128])
    w_in = din("w_in", [128, 128, KT * 128])
    w_out = din("w_out", [KT, 128, KT * 128])
    biasP = din("biasP", [128, NH, 640])
    biasS = din("biasS", [128, NH, 160])
    hb = din("hb", [128, 2])
    ident = din("ident", [128, 128])
    onesd = din("onesd", [128, 128])

    yp = dout("yp", [TP, D])
    ys = dout("ys", [TS, D])
    kp = dout("kp", [512, 2048])
    vp = dout("vp", [512, 2048])
    up = dout("up", [2, 2048])
    ksn = dout("ksn", [TS, 2048])
    vsn = dout("vsn", [TS, 2048])
    usn = dout("usn", [8, 2048])
    kscr = nc.dram_tensor("kscr", [NH, 128, 512], BF16, kind="Internal").ap()
    vscr = nc.dram_tensor("vscr", [NH, 128, 512], BF16, kind="Internal").ap()

    def sb(name, shape, dt=F32):
        return es.enter_context(nc.sbuf_tensor(name, shape, dt))

    ps = [es.enter_context(nc.psum_tensor("ps%d" % i, [128, 512], F32)) for i in range(8)]

    hT = sb("hT", [128, KT, NBLK], BF16)
    mixT = sb("mixT", [128, KT, NBLK], BF16)
    W = [sb("W%d" % i, [128, KT, 128], BF16) for i in range(NWSLOT)]
    xst = sb("xst", [128, D])
    gbc_sb = sb("gbc_sb", [128, D])
    biasP_sb = sb("biasP_sb", [128, NH, 640], BF16)
    biasS_sb = sb("biasS_sb", [128, NH, 160], BF16)
    kbuf = [sb("kbuf%d" % i, [128, 1024], BF16) for i in range(2)]
    vbuf = [sb("vbuf%d" % i, [128, 1024], BF16) for i in range(2)]
    ident_sb = sb("ident_sb", [128, 128])
    ones_bf = sb("ones_bf", [128, 128], BF16)
    cT = sb("cT", [128, KT, 5], BF16)
    gn_sb = sb("gn_sb", [128, KT])
    bada_sb = sb("bada_sb", [128, 96])
    cw_sb = sb("cw_sb", [128, 16, 3])
    cb_sb = sb("cb_sb", [128, 16])
    hb_sb = sb("hb_sb", [128, 2])
    mod = sb("mod", [128, 96, 5])
    gmod = sb("gmod", [128, KT, 5])
    ss8 = sb("ss8", [128, 8])
    ssv = sb("ssv", [128, 4])
    sqj = sb("sqj", [128, 512])
    ucarry = sb("ucarry", [128, 16, 2])
    uout = sb("uout", [128, 16, 8])
    xs_t = [sb("xs_t0", [128, 512])] * 2
    ubuf = [sb("ubuf%d" % i, [128, 544]) for i in range(2)]
    acc = [sb("acc0", [128, 512])] * 2
    a_t = [sb("a_t0", [128, 512])] * 2
    sz = [sb("sz%d" % i, [128, 512]) for i in range(2)]
    qT = [sb("qT%d" % i, [128, 512], BF16) for i in range(2)]
    kf = sb("kf", [128, 512])
    vf = sb("vf", [128, 512])
    kst = sb("kst", [128, 512])
    vst = sb("vst", [128, 512])
    s_sb = [sb("s_sb%d" % i, [128, 640]) for i in range(2)]
    pT = [sb("pT%d" % i, [128, 640], BF16) for i in range(2)]
    rden = sb("rden", [128, 512])
    t1 = sb("t1", [128, 512])
    kc = [sb("kc%d" % i, [128, 512]) for i in range(2)]
    kTc = [sb("kTc%d" % i, [128, 512], BF16) for i in range(2)]
    vc = [sb("vc%d" % i, [128, 512], BF16) for i in range(2)]
    vnew = [sb("vnew%d" % i, [32, 512], BF16) for i in range(2)]
    uoutp = sb("uoutp", [128, 2, 16])
    cc_sb = sb("cc_sb", [8, 128])
    ust = sb("ust", [32, 128])
    yT = ubuf
    xpc = [xs_t[0], acc[0]]
    ypc = [a_t[0], rden]
    xpc_id = [('xs_t', 0), ('acc', 0)]
    ypc_id = [('a_t', 0), 'rden']
    ssq = sb("ssq", [128, 4, 32])
    rstd2 = sb("rstd2", [128, 4])

    st = dict()

    def nxt(k, n):
        v = st.get(k, 0)
        st[k] = (v + 1) % n
        return v

    def ld(key, dst, src, wid, eng='sync'):
        P.dma(eng, key, writes=[wid], out=dst, in_=src)

    ld('c0', ident_sb[:], ident[:], 'ident')
    ld('c1', gn_sb[:], gn[:], 'gn')
    ld('c2', bada_sb[:], bada[:], 'bada')
    ld('c3', cw_sb[:], cw[:], 'cw')
    ld('c4', cb_sb[:], cbias[:], 'cb')
    ld('c5', hb_sb[:], hb[:], 'hb')
    ld('c6', gbc_sb[:], gbc[:], 'gbc')
    ld('c8', cT[:], c5T[:], 'cT', eng='pool')
    ld('c9', ones_bf[:], onesd[:], 'ones', eng='pool')
    ld('c10', biasP_sb[:], biasP[:], 'biasP', eng='pool')
    ld('c11', biasS_sb[:], biasS[:], 'biasS', eng='pool')

    scr_in_parts = [nc.dram_tensor("scr_in%d" % i, [16, 128, KT * 128], BF16, kind="Internal").ap()
                    for i in range(8)]
    scr_out_parts = [nc.dram_tensor("scr_out%d" % i, [16, 128, KT * 128], BF16, kind="Internal").ap()
                     for i in range(2)]

    class _Scr:
        def __init__(self, parts):
            self.parts = parts

        def __getitem__(self, cb):
            return self.parts[cb // 16][cb % 16]

    scr = {id(w_in): _Scr(scr_in_parts), id(w_out): _Scr(scr_out_parts)}
    cached = set()

    def colblock(wsrc, cb, rhs, rhs_ids, n0, N, bank):
        slot = nxt('wslot', NWSLOT)
        ck_ = (id(wsrc), cb)
        if ck_ in cached and st.get('use_cache', True):
            P.dma('pool', ('w', slot), reads=[('scr',) + ck_], writes=[('W', slot)],
                  out=W[slot][:].rearrange("p k c -> p (k c)"), in_=scr[id(wsrc)][cb])
        else:
            P.dma('pool', ('w', slot), writes=[('W', slot)],
                  out=W[slot][:].rearrange("p (a k) c -> p a (k c)", a=2),
                  in_=wsrc[cb].rearrange("p (a f) -> p a f", a=2))
            if id(wsrc) in scr and ck_ not in cached:
                P.dma('sync', ('ws', slot), reads=[('W', slot)], writes=[('scr',) + ck_],
                      out=scr[id(wsrc)][cb], in_=W[slot][:].rearrange("p k c -> p (k c)"))
                cached.add(ck_)
        for kt in range(KT):
            P.op('pe', 'matmul', reads=[('W', slot)] + rhs_ids, writes=[('ps', bank)],
                 out=ps[bank][:, 0:N], lhsT=W[slot][:, kt, :], rhs=rhs[:, kt, n0:n0 + N],
                 start=(kt == 0), stop=(kt == KT - 1))

    def ada_block(cb, bank):
        colblock(w_ada, cb, cT, ['cT'], 0, 5, bank)
        P.op('act', 'activation', reads=[('ps', bank), 'bada'], writes=[('mod', cb)],
             out=mod[:, cb, :], in_=ps[bank][:, 0:5], func=AF.Identity, bias=bada_sb[:, cb:cb + 1], scale=1.0)

    for cb in range(64):
        ada_block(cb, cb % 8)
    for j in range(5):
        P.op('dve', 'scalar_tensor_tensor', reads=[('mod', c) for c in range(32, 64)] + ['gn'],
             writes=[('gmod', j)],
             out=gmod[:, :, j], in0=mod[:, 32:64, j], scalar=1.0, in1=gn_sb[:, :], op0=ALU.add, op1=ALU.mult)

    def norm_front(xsrc, r0):
        P.dma('sync', 'xst', writes=[('xst', 0), ('xst', 1)], out=xst[:], in_=xsrc[r0:r0 + 128, :])
        for c in range(8):
            P.op('act', 'activation', reads=[('xst', c // 4)], writes=[('ss8', c)],
                 out=sqj[:], in_=xst[:, c * 512:(c + 1) * 512], func=AF.Square, accum_out=ss8[:, c:c + 1])
        P.op('dve', 'tensor_reduce', reads=[('ss8', c) for c in range(8)], writes=['ssv0'],
             out=ssv[:, 0:1], in_=ss8[:], axis=AX.X, op=ALU.add)
        P.op('dve', 'tensor_scalar', reads=['ssv0'], writes=['ssv1'],
             out=ssv[:, 1:2], in0=ssv[:, 0:1], scalar1=1.0 / D, scalar2=EPS, op0=ALU.mult, op1=ALU.add)
        P.op('act', 'activation', reads=['ssv1'], writes=['ssv2'],
             out=ssv[:, 2:3], in_=ssv[:, 1:2], func=AF.Sqrt)
        P.op('dve', 'reciprocal', reads=['ssv2'], writes=['ssv3'], out=ssv[:, 3:4], in_=ssv[:, 2:3])
        P.op('act', 'activation', reads=['ssv3', ('xst', 0)], writes=[('xst', 0)],
             out=xst[:, 0:2048], in_=xst[:, 0:2048], func=AF.Identity, scale=ssv[:, 3:4], bias=0.0)
        P.op('dve', 'tensor_scalar', reads=['ssv3', ('xst', 1)], writes=[('xst', 1)],
             out=xst[:, 2048:4096], in0=xst[:, 2048:4096], scalar1=ssv[:, 3:4], scalar2=None, op0=ALU.mult)

    def norm_back(tt, sample):
        for k4 in range(8):
            bank = nxt('tb', 8)
            for j in range(4):
                kt = k4 * 4 + j
                P.op('pe', 'transpose', reads=[('xst', kt // 16), 'ident'], writes=[('ps', bank)],
                     out=ps[bank][:, j * 128:(j + 1) * 128], in_=xst[:, kt * 128:(kt + 1) * 128],
                     identity=ident_sb[:])
            for j in range(4):
                kt = k4 * 4 + j
                segs = [(0, 128, 0)] if not sample else [(b * 32, 32, 1 + b) for b in range(4)]
                for (c0, cn, mj) in segs:
                    o = hT[:, kt, tt * 128 + c0: tt * 128 + c0 + cn]
                    i = ps[bank][:, j * 128 + c0: j * 128 + c0 + cn]
                    rd = [('ps', bank), ('gmod', mj), ('mod', kt)]
                    P.op('dve', 'tensor_scalar', reads=rd, writes=[('hT', tt, kt)],
                         out=o, in0=i, scalar1=gmod[:, kt, mj:mj + 1], scalar2=mod[:, kt, mj:mj + 1],
                         op0=ALU.mult, op1=ALU.add)

    def norm_stage(xsrc, row0, ntiles, sample):
        for tt in range(ntiles):
            norm_front(xsrc, row0 + tt * 128)
            norm_back(tt, sample)

    def norm_hooks(xsrc, row0, ntiles, sample):
        hooks = {}
        step = 32 // ntiles
        for tt in range(ntiles):
            hooks.setdefault(tt * step, []).append(
                (lambda tt: lambda: norm_front(xsrc, row0 + tt * 128))(tt))
            hooks.setdefault(tt * step + min(3, step - 1), []).append(
                (lambda tt: lambda: norm_back(tt, sample))(tt))
        return hooks

    def hT_ids(ntiles):
        return [('hT', tt, kt) for tt in range(ntiles) for kt in range(KT)]

    def conv_group(g, N, ntiles, kind):
        b0 = 4 * (g % 2)
        i2 = g % 2
        hid = hT_ids(ntiles)
        sample = (kind == 'sample')
        ub = ubuf[i2]
        if kind == 'halo':
            colblock(w_in, g, hT, hid, 480, 32, b0)
            colblock(w_in, 32 + g, hT, hid, 480, 32, b0 + 2)
            P.op('act', 'activation', reads=[('ps', b0)], writes=[('xs_t', 0)],
                 out=xs_t[i2][:, 0:32], in_=ps[b0][:, 0:32], func=AF.Copy)
            P.op('dve', 'tensor_tensor', reads=[('ps', b0 + 2), ('xs_t', 0)], writes=[('ubuf', i2)],
                 out=ub[:, 0:32], in0=ps[b0 + 2][:, 0:32], in1=xs_t[i2][:, 0:32], op=ALU.mult)
            P.op('dve', 'tensor_scalar', reads=[('ubuf', i2), 'hb'], writes=[('ucarry', g)],
                 out=ucarry[:, g, :], in0=ub[:, 30:32], scalar1=hb_sb[:, 1:2], scalar2=None, op0=ALU.mult)
            return
        if sample:
            ub3 = ub[:, 0:136].rearrange("p (b t) -> p b t", t=34)
            P.dma('sync', 'cc', writes=['cc'], out=cc_sb[:], in_=cc[:, g * 128:(g + 1) * 128])
            P.op('pe', 'transpose', reads=['cc', 'ident'], writes=[('ps', b0 + 3)],
                 out=ps[b0 + 3][:, 504:512], in_=cc_sb[0:8, :], identity=ident_sb[0:8, 0:8])
            P.op('act', 'activation', reads=[('ps', b0 + 3)], writes=[('ubuf', i2)],
                 out=ub3[:, :, 0:2], in_=ps[b0 + 3][:, 504:512].rearrange("p (b t) -> p b t", t=2),
                 func=AF.Copy)
        colblock(w_in, g, hT, hid, 0, N, b0)
        colblock(w_in, 32 + g, hT, hid, 0, N, b0 + 2)
        colblock(w_in, 16 + g, hT, hid, 0, N, b0 + 1)
        colblock(w_in, 48 + g, hT, hid, 0, N, b0 + 3)
        if not sample:
            uview = lambda off: ub[:, off:off + N]
            flat = lambda t: t[:, 0:N]
            P.op('act', 'activation', reads=[('ucarry', g)], writes=[('ubuf', i2)],
                 out=ub[:, 0:2], in_=ucarry[:, g, :], func=AF.Copy)
        else:
            ub3 = ub[:, 0:136].rearrange("p (b t) -> p b t", t=34)
            uview = lambda off: ub3[:, :, off:off + 32]
            flat = lambda t: t[:, 0:N].rearrange("p (b t) -> p b t", t=32)
        P.op('act', 'activation', reads=[('ps', b0)], writes=[('xs_t', 0)],
             out=xs_t[i2][:, 0:N], in_=ps[b0][:, 0:N], func=AF.Copy)
        P.op('dve', 'tensor_tensor', reads=[('ps', b0 + 2), ('xs_t', 0), ('ubuf', i2)], writes=[('ubuf', i2)],
             out=uview(2), in0=flat(ps[b0 + 2]), in1=flat(xs_t[i2]), op=ALU.mult)
        P.op('dve', 'tensor_scalar', reads=[('ubuf', i2), 'cw', 'cb'], writes=[('acc', 0)],
             out=flat(acc[i2]), in0=uview(0), scalar1=cw_sb[:, g, 0:1], scalar2=cb_sb[:, g:g + 1],
             op0=ALU.mult, op1=ALU.add)
        for tap in (1, 2):
            P.op('dve', 'scalar_tensor_tensor', reads=[('ubuf', i2), 'cw', ('acc', 0)], writes=[('acc', 0)],
                 out=flat(acc[i2]), in0=uview(tap), scalar=cw_sb[:, g, tap:tap + 1], in1=flat(acc[i2]),
                 op0=ALU.mult, op1=ALU.add)
        P.op('dve', 'tensor_tensor', reads=[('ps', b0 + 1), ('acc', 0)], writes=[('a_t', 0)],
             out=a_t[i2][:, 0:N], in0=ps[b0 + 1][:, 0:N], in1=acc[i2][:, 0:N], op=ALU.mult)
        P.op('act', 'activation', reads=[('ps', b0 + 3)], writes=[('sz', i2)],
             out=sz[i2][:, 0:N], in_=ps[b0 + 3][:, 0:N], func=AF.Silu)
        P.op('dve', 'tensor_tensor', reads=[('a_t', 0), ('sz', i2)], writes=[('mixT', g)],
             out=mixT[:, g, 0:N], in0=a_t[i2][:, 0:N], in1=sz[i2][:, 0:N], op=ALU.mult)
        if not sample:
            P.op('act', 'activation', reads=[('ubuf', i2)], writes=[('ucarry', g)],
                 out=ucarry[:, g, :], in_=ub[:, N:N + 2], func=AF.Copy)
        else:
            P.op('act', 'activation', reads=[('ubuf', i2)], writes=[('uout', g)],
                 out=uout[:, g, :].rearrange("p (b t) -> p b t", t=2), in_=ub3[:, :, 32:34], func=AF.Copy)

    def head_proj_steps(h, N, ntiles, kind, pidx):
        hid = hT_ids(ntiles)
        i2 = h % 2
        sample = (kind == 'sample')
        kb, vb = kbuf[i2], vbuf[i2]
        out_pass = sample or (kind == 'prompt' and pidx == NPASS - 1)
        seg = 32 if sample else 128
        nseg = N // seg

        def step_q():
            if kind == 'prompt':
                P.dma('sync', ('kprev', i2), reads=[('kscr', h)], writes=[('kb_prev', i2)],
                      out=kb[:, 0:512], in_=kscr[h])
                P.dma('sync', ('vprev', i2), reads=[('vscr', h)], writes=[('vb_prev', i2)],
                      out=vb[:, 0:512], in_=vscr[h])
            if kind != 'halo':
                colblock(w_in, 64 + h, hT, hid, 0, N, 0)
                P.op('act', 'activation', reads=[('ps', 0)], writes=[('qT', i2)],
                     out=qT[i2][:, 0:N], in_=ps[0][:, 0:N], func=AF.Copy, scale=ATTN_SCALE)

        def step_k():
            colblock(w_in, 80 + h, hT, hid, 0, N, 1)
            if out_pass:
                P.op('dve', 'tensor_copy', reads=[('ps', 1)], writes=['kf'], out=kf[:, 0:N], in_=ps[1][:, 0:N])
                P.op('act', 'activation', reads=['kf'], writes=[('kb_cur', i2)],
                     out=kb[:, 512:512 + N], in_=kf[:, 0:N], func=AF.Copy)
            else:
                P.op('act', 'activation', reads=[('ps', 1)], writes=[('kb_cur', i2)],
                     out=kb[:, 512:512 + N], in_=ps[1][:, 0:N], func=AF.Copy)

        def step_v():
            colblock(w_in, 96 + h, hT, hid, 0, N, 2)
            P.op('dve', 'tensor_copy', reads=[('ps', 2)], writes=['vf'], out=vf[:, 0:N], in_=ps[2][:, 0:N])

        def step_z():
            if kind != 'halo':
                colblock(w_in, 112 + h, hT, hid, 0, N, 3)
                P.op('act', 'activation', reads=[('ps', 3)], writes=[('sz', i2)],
                     out=sz[i2][:, 0:N], in_=ps[3][:, 0:N], func=AF.Silu)

        def step_post():
            for s in range(nseg):
                P.op('pe', 'transpose', reads=['vf', 'ident'], writes=[('ps', 4)],
                     out=ps[4][0:seg, s * 128:(s + 1) * 128], in_=vf[:, s * seg:(s + 1) * seg],
                     identity=ident_sb[:])
            vsrc, vsrc_id = ps[4], ('ps', 4)
            if out_pass:
                P.op('dve', 'tensor_copy', reads=[('ps', 4)], writes=['vst'],
                     out=vst[0:seg, :], in_=ps[4][0:seg, 0:512])
                vsrc, vsrc_id = vst, 'vst'
            if not sample:
                P.op('act', 'activation', reads=[vsrc_id], writes=[('vb_cur', i2)],
                     out=vb[:, 512:1024], in_=vsrc[:, 0:512], func=AF.Copy)
            else:
                P.op('act', 'activation', reads=[vsrc_id], writes=[('vnew', i2)],
                     out=vnew[i2][0:32, :], in_=vsrc[0:32, 0:512], func=AF.Copy)
            if out_pass:
                vdst = vsn if sample else vp
                kdst = ksn if sample else kp
                pat = "(b t) d -> t b d" if sample else "(tt p) d -> p tt d"
                kw = dict(t=32) if sample else dict(p=128)
                P.dma('sync', 'vst', reads=['vst'],
                      out=vdst[:, h * 128:(h + 1) * 128].rearrange(pat, **kw),
                      in_=vst[0:seg, :].rearrange("p (a d) -> p a d", d=128))
                for s in range(nseg):
                    P.op('pe', 'transpose', reads=['kf', 'ident'], writes=[('ps', 5)],
                         out=ps[5][0:seg, s * 128:(s + 1) * 128], in_=kf[:, s * seg:(s + 1) * seg],
                         identity=ident_sb[:])
                P.op('act', 'activation', reads=[('ps', 5)], writes=['kst'],
                     out=kst[0:seg, :], in_=ps[5][0:seg, 0:512], func=AF.Copy)
                P.dma('sync', 'kst', reads=['kst'],
                      out=kdst[:, h * 128:(h + 1) * 128].rearrange(pat, **kw),
                      in_=kst[0:seg, :].rearrange("p (a d) -> p a d", d=128))
            if kind == 'halo' or (kind == 'prompt' and pidx < NPASS - 1):
                P.dma('sync', ('kcur', i2), reads=[('kb_cur', i2)], writes=[('kscr', h)],
                      out=kscr[h], in_=kb[:, 512:1024])
                P.dma('sync', ('vcur', i2), reads=[('vb_cur', i2)], writes=[('vscr', h)],
                      out=vscr[h], in_=vb[:, 512:1024])

        return [step_q, step_k, step_v, step_z, step_post]

    def head_proj(h, N, ntiles, kind, pidx):
        for s in head_proj_steps(h, N, ntiles, kind, pidx):
            s()

    def attn_epilogue(h, N):
        i2 = h % 2
        P.op('dve', 'reciprocal', reads=[('ps', 7)], writes=['rden'], out=rden[:, 0:N], in_=ps[7][:, 0:N])
        P.op('dve', 'tensor_tensor', reads=[('ps', 6), 'rden'], writes=['t1'],
             out=t1[:, 0:N], in0=ps[6][:, 0:N], in1=rden[:, 0:N], op=ALU.mult)
        P.op('dve', 'tensor_tensor', reads=['t1', ('sz', i2)], writes=[('mixT', 16 + h)],
             out=mixT[:, 16 + h, 0:N], in0=t1[:, 0:N], in1=sz[i2][:, 0:N], op=ALU.mult)

    def head_attn_steps(h, pidx):
        i2 = h % 2
        kb, vb = kbuf[i2], vbuf[i2]
        kv_ids = [('kb_prev', i2), ('kb_cur', i2)]
        vv_ids = [('vb_prev', i2), ('vb_cur', i2)]
        sis = {}

        def qk(j):
            si = nxt('si', 2)
            sis[j] = si
            for t in range(5):
                bank, col = (4, t * 128) if t < 4 else (5, 0)
                P.op('pe', 'matmul', reads=kv_ids + [('qT', i2)], writes=[('ps', bank)],
                     out=ps[bank][:, col:col + 128], lhsT=kb[:, (j + t) * 128:(j + t + 1) * 128],
                     rhs=qT[i2][:, j * 128:(j + 1) * 128], start=True, stop=True)
            nhalo = max(0, 4 - j) if pidx == 0 else 0
            segs = []
            if nhalo > 0:
                segs.append((0, min(nhalo, 4), True))
            if nhalo < 4:
                segs.append((nhalo, 4, False))
            segs.append((4, 5, False))
            for (t0, t1_, halo) in segs:
                bank, col = (4, t0 * 128) if t0 < 4 else (5, 0)
                n = (t1_ - t0) * 128
                o = s_sb[si][:, t0 * 128:t0 * 128 + n]
                i = ps[bank][:, col:col + n]
                bi = biasP_sb[:, h, t0 * 128:t0 * 128 + n]
                if halo:
                    P.op('dve', 'scalar_tensor_tensor', reads=[('ps', bank), 'biasP', 'hb'], writes=[('s_sb', si)],
                         out=o, in0=i, scalar=hb_sb[:, 0:1], in1=bi, op0=ALU.add, op1=ALU.add)
                else:
                    P.op('dve', 'tensor_tensor', reads=[('ps', bank), 'biasP'], writes=[('s_sb', si)],
                         out=o, in0=i, in1=bi, op=ALU.add)
            P.op('act', 'activation', reads=[('s_sb', si)], writes=[('pT', si)],
                 out=pT[si][:], in_=s_sb[si][:], func=AF.Exp)

        def pv(j):
            si = sis[j]
            for t in range(5):
                P.op('pe', 'matmul', reads=vv_ids + [('pT', si)], writes=[('ps', 6)],
                     out=ps[6][:, j * 128:(j + 1) * 128], lhsT=vb[:, (j + t) * 128:(j + t + 1) * 128],
                     rhs=pT[si][:, t * 128:(t + 1) * 128], start=(t == 0), stop=(t == 4))
                P.op('pe', 'matmul', reads=['ones', ('pT', si)], writes=[('ps', 7)],
                     out=ps[7][:, j * 128:(j + 1) * 128], lhsT=ones_bf[:],
                     rhs=pT[si][:, t * 128:(t + 1) * 128], start=(t == 0), stop=(t == 4))

        return qk, pv

    def heads_prompt(pidx):
        head_proj(0, NBLK, 4, 'prompt', pidx)
        for h in range(NH):
            nx = head_proj_steps(h + 1, NBLK, 4, 'prompt', pidx) if h + 1 < NH else [lambda: None] * 5
            qk, pv = head_attn_steps(h, pidx)
            nx[0]()
            qk(0)
            nx[1]()
            pv(0)
            qk(1)
            nx[2]()
            pv(1)
            qk(2)
            nx[3]()
            pv(2)
            qk(3)
            nx[4]()
            pv(3)
            attn_epilogue(h, NBLK)

    def head_attn_sample_steps(h):
        i2 = h % 2
        kb = kbuf[i2]
        state = {}

        def stage_a(b):
            ci = nxt('ci', 2)
            si = nxt('si', 2)
            state[b] = (ci, si)
            P.dma('sync', ('kc', ci), writes=[('kc', ci)],
                  out=kc[ci][:].rearrange("p (t d) -> p t d", d=128),
                  in_=ck[b, :, h, :].rearrange("(t p) d -> p t d", p=128))
            P.dma('pool', ('vc', ci), writes=[('vc', ci)],
                  out=vc[ci][:].rearrange("p (t d) -> p t d", d=128),
                  in_=cv[b, :, h, :].rearrange("(t p) d -> p t d", p=128))
            for t in range(4):
                P.op('pe', 'transpose', reads=[('kc', ci), 'ident'], writes=[('ps', 5)],
                     out=ps[5][:, t * 128:(t + 1) * 128], in_=kc[ci][:, t * 128:(t + 1) * 128],
                     identity=ident_sb[:])
            P.op('act', 'activation', reads=[('ps', 5)], writes=[('kTc', ci)],
                 out=kTc[ci][:], in_=ps[5][:, 0:512], func=AF.Copy)
            qs = qT[i2][:, b * 32:(b + 1) * 32]
            for t in range(4):
                P.op('pe', 'matmul', reads=[('kTc', ci), ('qT', i2)], writes=[('ps', 4)],
                     out=ps[4][:, t * 32:(t + 1) * 32], lhsT=kTc[ci][:, t * 128:(t + 1) * 128], rhs=qs,
                     start=True, stop=True)
            P.op('pe', 'matmul', reads=[('kb_cur', i2), ('qT', i2)], writes=[('ps', 4)],
                 out=ps[4][0:32, 128:160], lhsT=kb[:, 512 + b * 32:512 + (b + 1) * 32], rhs=qs,
                 start=True, stop=True)
            P.op('dve', 'tensor_tensor', reads=[('ps', 4), 'biasS'], writes=[('s_sb', si)],
                 out=s_sb[si][:, 0:128], in0=ps[4][:, 0:128], in1=biasS_sb[:, h, 0:128], op=ALU.add)
            P.op('dve', 'tensor_tensor', reads=[('ps', 4), 'biasS'], writes=[('s_sb', si)],
                 out=s_sb[si][0:32, 128:160], in0=ps[4][0:32, 128:160], in1=biasS_sb[0:32, h, 128:160], op=ALU.add)
            P.op('act', 'activation', reads=[('s_sb', si)], writes=[('pT', si)],
                 out=pT[si][:, 0:128], in_=s_sb[si][:, 0:128], func=AF.Exp)
            P.op('act', 'activation', reads=[('s_sb', si)], writes=[('pT', si)],
                 out=pT[si][0:32, 128:160], in_=s_sb[si][0:32, 128:160], func=AF.Exp)

        def stage_b(b):
            ci, si = state[b]
            for t in range(5):
                if t < 4:
                    lv = vc[ci][:, t * 128:(t + 1) * 128]
                    lo = ones_bf[:]
                    r = pT[si][:, t * 32:(t + 1) * 32]
                else:
                    lv = vnew[i2][0:32, b * 128:(b + 1) * 128]
                    lo = ones_bf[0:32, :]
                    r = pT[si][0:32, 128:160]
                P.op('pe', 'matmul', reads=[('vc', ci), ('vnew', i2), ('pT', si)], writes=[('ps', 6)],
                     out=ps[6][:, b * 32:(b + 1) * 32], lhsT=lv, rhs=r, start=(t == 0), stop=(t == 4))
                P.op('pe', 'matmul', reads=['ones', ('pT', si)], writes=[('ps', 7)],
                     out=ps[7][:, b * 32:(b + 1) * 32], lhsT=lo, rhs=r, start=(t == 0), stop=(t == 4))

        return stage_a, stage_b

    def heads_sample():
        head_proj(0, TS, 1, 'sample', -1)
        for h in range(NH):
            nx = head_proj_steps(h + 1, TS, 1, 'sample', -1) if h + 1 < NH else [lambda: None] * 5
            sa, sbb = head_attn_sample_steps(h)
            nx[0]()
            sa(0)
            nx[1]()
            sbb(0)
            sa(1)
            nx[2]()
            sbb(1)
            sa(2)
            nx[3]()
            sbb(2)
            sa(3)
            nx[4]()
            sbb(3)
            attn_epilogue(h, TS)

    def out_proj(N, ntiles, xsrc, xrow0, ydst, yrow0, sample, hooks=None):
        mids = [('mixT', k) for k in range(KT)]

        def part_b(cb, yi):
            tbk = 4 + (cb % 4)
            for tt in range(ntiles):
                P.op('pe', 'transpose', reads=[('ubuf', yi), 'ident'], writes=[('ps', tbk)],
                     out=ps[tbk][:, tt * 128:(tt + 1) * 128], in_=yT[yi][:, tt * 128:(tt + 1) * 128],
                     identity=ident_sb[:])
            P.dma('sync', ('xpc', yi), writes=[xpc_id[yi]],
                  out=xpc[yi][:, 0:N].rearrange("p (a c) -> p a c", c=128),
                  in_=xsrc[xrow0:xrow0 + N, cb * 128:(cb + 1) * 128].rearrange("(a p) c -> p a c", p=128))
            P.op('dve', 'tensor_tensor', reads=[('ps', tbk), xpc_id[yi]], writes=[ypc_id[yi]],
                 out=ypc[yi][:, 0:N], in0=ps[tbk][:, 0:N], in1=xpc[yi][:, 0:N], op=ALU.add)
            for tt in range(ntiles):
                P.op('act', 'activation', reads=[ypc_id[yi]], writes=[('ssq', tt, cb)],
                     out=sqj[:, 0:128], in_=ypc[yi][:, tt * 128:(tt + 1) * 128], func=AF.Square,
                     accum_out=ssq[:, tt, cb:cb + 1])
            P.dma('sync', ('ypc', yi), reads=[ypc_id[yi]], writes=[('ydram', cb)],
                  out=ydst[yrow0:yrow0 + N, cb * 128:(cb + 1) * 128].rearrange("(a p) c -> p a c", p=128),
                  in_=ypc[yi][:, 0:N].rearrange("p (a c) -> p a c", c=128))

        pend = None
        for cb in range(KT):
            for hk in (hooks or {}).get(cb, []):
                hk()
            bank = cb % 4
            yi = nxt('yi', 2)
            colblock(w_out, cb, mixT, mids, 0, N, bank)
            segs = [(0, N, 0)] if not sample else [(b * 32, 32, 1 + b) for b in range(4)]
            for (c0, cn, mj) in segs:
                P.op('act', 'activation', reads=[('ps', bank), ('mod', 64 + cb)], writes=[('ubuf', yi)],
                     out=yT[yi][:, c0:c0 + cn], in_=ps[bank][:, c0:c0 + cn], func=AF.Identity,
                     scale=mod[:, 64 + cb, mj:mj + 1], bias=0.0)
            if pend is not None:
                pend()
            pend = (lambda cb, yi: lambda: part_b(cb, yi))(cb, yi)
        pend()
        for tt in range(ntiles):
            r2 = rstd2[:, tt:tt + 1]
            P.op('dve', 'tensor_reduce', reads=[('ssq', tt, c) for c in range(KT)], writes=[('r2a', tt)],
                 out=r2, in_=ssq[:, tt, :], axis=AX.X, op=ALU.add)
            P.op('dve', 'tensor_scalar', reads=[('r2a', tt)], writes=[('r2a', tt)],
                 out=r2, in0=r2, scalar1=1.0 / D, scalar2=EPS, op0=ALU.mult, op1=ALU.add)
            P.op('act', 'activation', reads=[('r2a', tt)], writes=[('r2a', tt)], out=r2, in_=r2, func=AF.Sqrt)
            P.op('dve', 'reciprocal', reads=[('r2a', tt)], writes=[('r2a', tt)], out=r2, in_=r2)
            r0 = yrow0 + tt * 128
            P.dma('sync', 'xst', reads=[('ydram', c) for c in range(KT)], writes=[('xst', 0), ('xst', 1)],
                  out=xst[:], in_=ydst[r0:r0 + 128, :])
            P.op('dve', 'scalar_tensor_tensor', reads=[('xst', 0), ('xst', 1), ('r2a', tt), 'gbc'],
                 writes=[('xst', 0), ('xst', 1)],
                 out=xst[:], in0=xst[:], scalar=r2, in1=gbc_sb[:], op0=ALU.mult, op1=ALU.mult)
            P.dma('sync', 'yfin', reads=[('xst', 0), ('xst', 1)], writes=[('ydram', c) for c in range(KT)],
                  out=ydst[r0:r0 + 128, :], in_=xst[:])

    def program():
        norm_stage(xp, 0, 4, False)
        for g in range(16):
            conv_group(g, NBLK, 4, 'halo')
        for h in range(NH):
            head_proj(h, NBLK, 4, 'halo', -1)
        norm_stage(xp, HALO, 4, False)
        for pidx in range(NPASS):
            row0 = HALO + pidx * NBLK
            for g in range(16):
                if pidx == 0:
                    ada_block(64 + 2 * g, 4 * (g % 2))
                    ada_block(64 + 2 * g + 1, 4 * (g % 2) + 1)
                conv_group(g, NBLK, 4, 'prompt')
            heads_prompt(pidx)
            if pidx + 1 < NPASS:
                hooks = norm_hooks(xp, row0 + NBLK, 4, False)
            else:
                hooks = norm_hooks(xs, 0, 1, True)
            out_proj(NBLK, 4, xp, row0, yp, pidx * NBLK, False, hooks)
            if pidx == NPASS - 1:
                for g in range(16):
                    P.op('act', 'activation', reads=[('ucarry', g)], writes=[('uoutp', g)],
                         out=uoutp[:, :, g], in_=ucarry[:, g, :], func=AF.Copy)
                P.op('pe', 'transpose', reads=[('uoutp', g) for g in range(16)] + ['ident'], writes=[('ps', 0)],
                     out=ps[0][0:32, 0:128], in_=uoutp[:, :, :], identity=ident_sb[:])
                P.op('act', 'activation', reads=[('ps', 0)], writes=['ust'],
                     out=ust[0:32, 0:128], in_=ps[0][0:32, 0:128], func=AF.Copy)
                for t in range(2):
                    P.dma('sync', 'ust', reads=['ust'],
                          out=up[t, :].rearrange("(g p) -> g p", p=128), in_=ust[t * 16:(t + 1) * 16, 0:128])
        for g in range(16):
            conv_group(g, TS, 1, 'sample')
        heads_sample()
        out_proj(TS, 1, xs, 0, ys, 0, True)
        for g4 in range(4):
            P.op('pe', 'transpose', reads=[('uout', g) for g in range(16)] + ['ident'], writes=[('ps', 0)],
                 out=ps[0][0:32, g4 * 128:(g4 + 1) * 128], in_=uout[:, g4 * 4:(g4 + 1) * 4, :],
                 identity=ident_sb[:])
        P.op('act', 'activation', reads=[('ps', 0)], writes=['kst'],
             out=kst[0:32, 0:512], in_=ps[0][0:32, 0:512], func=AF.Copy)
        for g in range(16):
            g4, gl = g // 4, g % 4
            P.dma('sync', 'kst', reads=['kst'],
                  out=usn[:, g * 128:(g + 1) * 128], in_=kst[gl * 8:(gl + 1) * 8, g4 * 128:(g4 + 1) * 128])

    program()
    P.emit(nc, es)
    es.close()
    return nc


def _bias_tiles(rel_bias):
    rb = rel_bias[0]
    p = np.arange(128)[:, None, None]
    t = np.arange(5)[None, :, None]
    q = np.arange(128)[None, None, :]
    rel = q - ((t - 4) * 128 + p)
    idx = np.clip(rel, -256, 256) + 256
    bp = rb[:, idx]
    qc = q // 64
    kc = ((t - 4) * 128 + p) // 64
    visible = (kc <= qc) & (kc >= qc - 8)
    bp = np.where(visible[None], bp, np.float32(NEG)).astype(np.float32)
    biasP = np.ascontiguousarray(bp.transpose(1, 0, 2, 3)).reshape(128, 16, 640)
    j = (np.arange(5)[None, :, None] * 128 + np.arange(128)[:, None, None])
    tq = np.arange(32)[None, None, :]
    rel_s = 512 + tq - j
    idx_s = np.clip(rel_s, -256, 256) + 256
    bs = rb[:, idx_s]
    bs = np.where((j < 544)[None], bs, np.float32(NEG)).astype(np.float32)
    biasS = np.ascontiguousarray(bs.transpose(1, 0, 2, 3)).reshape(128, 16, 160)
    return biasP, biasS


_NC_CACHE = {}


def kernel(x_prompt, x_sample, cache_k, cache_v, cache_conv, c_prompt, c_sample,
           g_norm, w_ada, b_ada, w_in, conv_w, conv_b, rel_bias, w_out, g_final):
    f32 = np.float32
    x_prompt = np.asarray(x_prompt, f32); x_sample = np.asarray(x_sample, f32)
    cache_k = np.asarray(cache_k, f32); cache_v = np.asarray(cache_v, f32)
    cache_conv = np.asarray(cache_conv, f32)
    def blk(w):
        n = w.shape[1] // 128
        return np.ascontiguousarray(w.reshape(KT, 128, n, 128).transpose(2, 1, 0, 3)).reshape(n, 128, KT * 128)
    w_ada2 = blk(np.asarray(w_ada, f32)[0])
    w_in2 = blk(np.asarray(w_in, f32)[0])
    w_out2 = blk(np.asarray(w_out, f32)[0])
    fm = lambda v, n: np.ascontiguousarray(np.asarray(v, f32).reshape(n, 128).T)
    gn = fm(g_norm[0], 32)
    bada = fm(b_ada[0], 96)
    cw = np.ascontiguousarray(np.asarray(conv_w, f32)[0].reshape(3, 16, 128).transpose(2, 1, 0))
    cbias = fm(conv_b[0], 16)
    gbc = np.ascontiguousarray(np.broadcast_to(np.asarray(g_final, f32)[None, :], (128, D)))
    biasP, biasS = _bias_tiles(np.asarray(rel_bias, f32))
    ident = np.eye(128, dtype=f32)
    onesd = np.ones((128, 128), f32)
    xfull = np.concatenate([np.zeros((HALO, D), f32), x_prompt[0]], axis=0)

    in_maps = []
    for c in range(NCORES):
        c5 = np.concatenate([np.asarray(c_prompt, f32), np.asarray(c_sample, f32)[4 * c:4 * c + 4]], axis=0)
        c5T = np.ascontiguousarray(c5.T.reshape(32, 128, 5).transpose(1, 0, 2))
        hbv = np.zeros((128, 2), f32)
        hbv[:, 0] = NEG if c == 0 else 0.0
        hbv[:, 1] = 0.0 if c == 0 else 1.0
        in_maps.append(dict(
            xp=np.ascontiguousarray(xfull[c * TP: c * TP + HALO + TP]),
            xs=np.ascontiguousarray(x_sample[4 * c:4 * c + 4].reshape(TS, D)),
            ck=np.ascontiguousarray(cache_k[0, 4 * c:4 * c + 4]),
            cv=np.ascontiguousarray(cache_v[0, 4 * c:4 * c + 4]),
            cc=np.ascontiguousarray(cache_conv[0, 4 * c:4 * c + 4].reshape(8, 2048)),
            c5T=c5T, gn=gn, bada=bada, cw=cw, cbias=cbias, gbc=gbc,
            w_ada=w_ada2, w_in=w_in2, w_out=w_out2, biasP=biasP, biasS=biasS, hb=hbv,
            ident=ident, onesd=onesd))
    if 'nc' not in _NC_CACHE:
        _NC_CACHE['nc'] = build_nc()
    res = run_bass_kernel_spmd(_NC_CACHE['nc'], in_maps, core_ids=list(range(NCORES)))
    R = res.results
    y_prompt = np.concatenate([R[c]["yp"] for c in range(NCORES)], axis=0)[None]
    y_sample = np.concatenate([R[c]["ys"].reshape(4, 32, D) for c in range(NCORES)], axis=0)
    new_k_prompt = R[NCORES - 1]["kp"].reshape(1, 1, 512, NH, 128)
    new_v_prompt = R[NCORES - 1]["vp"].reshape(1, 1, 512, NH, 128)
    new_conv_prompt = R[NCORES - 1]["up"].reshape(1, 1, 2, 2048)
    new_k_sample = np.concatenate([R[c]["ksn"].reshape(4, 32, NH, 128) for c in range(NCORES)], axis=0)[None]
    new_v_sample = np.concatenate([R[c]["vsn"].reshape(4, 32, NH, 128) for c in range(NCORES)], axis=0)[None]
    new_conv_sample = np.concatenate([R[c]["usn"].reshape(4, 2, 2048) for c in range(NCORES)], axis=0)[None]
    return tuple(np.ascontiguousarray(a, dtype=f32) for a in (
        y_prompt, y_sample, new_k_prompt, new_v_prompt, new_conv_prompt,
        new_k_sample, new_v_sample, new_conv_sample))
```
